# Optimizing a Trainium2 kernel written in Bass

```python
import jax, jax.numpy as jnp
from jax import lax
import numpy as np

D_MODEL = 1024
BATCH = 2
SEQ = 8192
DEPTH = 1

LRU_WIDTH = D_MODEL
LRU_BLOCKS = 16
LRU_BLOCK_W = LRU_WIDTH // LRU_BLOCKS
LRU_C = 8.0
CONV_W = 4
ATT_HEADS = 16
ATT_HEAD_DIM = 64
ATT_WIDTH = ATT_HEADS * ATT_HEAD_DIM
Q_BLOCK = 128
D_FF = 4 * D_MODEL
PLE_DIM = 256
NORM_EPS = 1e-6

IN_SPLITS = [LRU_WIDTH, LRU_WIDTH, ATT_WIDTH, ATT_WIDTH, ATT_WIDTH, D_MODEL, D_MODEL]
IN_WIDTH = sum(IN_SPLITS)

kernel_name = 'hawk_stickbreak_hybrid'


def rmsnorm(x, g):
    x32 = x.astype(jnp.float32)
    y = x32 * lax.rsqrt(jnp.mean(x32 * x32, axis=-1, keepdims=True) + NORM_EPS)
    return (y * g.astype(jnp.float32)).astype(x.dtype)


def causal_depthwise_conv(x, w, b):
    S = x.shape[1]
    xp = jnp.pad(x, ((0, 0), (CONV_W - 1, 0), (0, 0)))
    y = b
    for k in range(CONV_W):
        y = y + w[k] * xp[:, k:k + S]
    return y


def rg_lru(x, w_r, b_r, w_i, b_i, lam):
    B, S, W = x.shape
    xb = x.reshape(B, S, LRU_BLOCKS, LRU_BLOCK_W)
    r = jax.nn.sigmoid((jnp.einsum('bsnc,ncd->bsnd', xb, w_r).reshape(B, S, W) + b_r).astype(jnp.float32))
    i = jax.nn.sigmoid((jnp.einsum('bsnc,ncd->bsnd', xb, w_i).reshape(B, S, W) + b_i).astype(jnp.float32))
    log_a = -LRU_C * r * jax.nn.softplus(-lam.astype(jnp.float32))
    a = jnp.exp(log_a)
    mult = jnp.sqrt(-jnp.expm1(2.0 * log_a))
    u = mult * (i * x.astype(jnp.float32))

    def combine(left, right):
        a_l, u_l = left
        a_r, u_r = right
        return a_l * a_r, a_r * u_l + u_r

    _, h = lax.associative_scan(combine, (a, u), axis=1)
    return h.astype(x.dtype)


def stick_breaking_attention(q, k, v):
    B, H, S, Dh = q.shape
    n_blk = S // Q_BLOCK
    scale = Dh ** -0.5
    q_blocks = q.reshape(B, H, n_blk, Q_BLOCK, Dh).transpose(2, 0, 1, 3, 4)
    key_pos = jnp.arange(S)

    def one_block(args):
        q_blk, blk = args
        z = jnp.einsum('bhqd,bhkd->bhqk', q_blk, k).astype(jnp.float32) * scale
        q_pos = blk * Q_BLOCK + jnp.arange(Q_BLOCK)
        causal = key_pos[None, :] < q_pos[:, None]
        log_1m = jnp.where(causal, jax.nn.log_sigmoid(-z), 0.0)
        suffix = lax.cumsum(log_1m, axis=3, reverse=True) - log_1m
        att = jnp.where(causal, jnp.exp(jax.nn.log_sigmoid(z) + suffix), 0.0)
        return jnp.einsum('bhqk,bhkd->bhqd', att.astype(v.dtype), v)

    out = lax.map(one_block, (q_blocks, jnp.arange(n_blk)))
    return out.transpose(1, 2, 0, 3, 4).reshape(B, H, S, Dh)


def setup_inputs(seed: int = 0) -> dict:
    key = jax.random.key(seed)
    ks = jax.random.split(key, 24)
    f32 = jnp.float32

    def nrm(k, shape, fan_in):
        return jax.random.normal(k, shape, f32) * (fan_in ** -0.5)

    def gain(k, shape):
        return 1.0 + 0.02 * jax.random.normal(k, shape, f32)

    u = jax.random.uniform(ks[10], (DEPTH, LRU_WIDTH), f32, 0.9, 0.999)
    a0 = u ** (1.0 / LRU_C)
    lam = jnp.log(a0) - jnp.log1p(-a0)
    return {
        'x': jax.random.normal(ks[0], (BATCH, SEQ, D_MODEL), f32),
        'p': jax.random.normal(ks[1], (DEPTH, BATCH, SEQ, PLE_DIM), f32),
        'norm_mix_g': gain(ks[2], (DEPTH, D_MODEL)),
        'w_in': nrm(ks[3], (DEPTH, D_MODEL, IN_WIDTH), D_MODEL),
        'conv_w': nrm(ks[4], (DEPTH, CONV_W, LRU_WIDTH), CONV_W),
        'conv_b': 0.02 * jax.random.normal(ks[5], (DEPTH, LRU_WIDTH), f32),
        'w_rgate': nrm(ks[6], (DEPTH, LRU_BLOCKS, LRU_BLOCK_W, LRU_BLOCK_W), LRU_BLOCK_W),
        'b_rgate': 0.02 * jax.random.normal(ks[7], (DEPTH, LRU_WIDTH), f32),
        'w_igate': nrm(ks[8], (DEPTH, LRU_BLOCKS, LRU_BLOCK_W, LRU_BLOCK_W), LRU_BLOCK_W),
        'b_igate': 0.02 * jax.random.normal(ks[9], (DEPTH, LRU_WIDTH), f32),
        'lru_lambda': lam,
        'w_br_lru': nrm(ks[11], (DEPTH, LRU_WIDTH, D_MODEL), LRU_WIDTH),
        'w_br_att': nrm(ks[12], (DEPTH, ATT_WIDTH, D_MODEL), ATT_WIDTH),
        'w_out': nrm(ks[13], (DEPTH, D_MODEL, D_MODEL), D_MODEL),
        'norm_mlp_g': gain(ks[14], (DEPTH, D_MODEL)),
        'w_mlp_up': nrm(ks[15], (DEPTH, D_MODEL, D_FF), D_MODEL),
        'w_mlp_down': nrm(ks[16], (DEPTH, D_FF, D_MODEL), D_FF),
        'norm_ple_g': gain(ks[17], (DEPTH, D_MODEL)),
        'w_ple_gate': nrm(ks[18], (DEPTH, D_MODEL, D_MODEL), D_MODEL),
        'w_ple': nrm(ks[19], (DEPTH, PLE_DIM, D_MODEL), PLE_DIM),
        'norm_final_g': gain(ks[20], (D_MODEL,)),
    }


def reference(x, p, norm_mix_g, w_in, conv_w, conv_b, w_rgate, b_rgate, w_igate, b_igate,
              lru_lambda, w_br_lru, w_br_att, w_out, norm_mlp_g, w_mlp_up, w_mlp_down,
              norm_ple_g, w_ple_gate, w_ple, norm_final_g):
    B, S, _ = x.shape
    split_idx = [int(s) for s in np.cumsum(IN_SPLITS)[:-1]]
    for l in range(DEPTH):
        h = rmsnorm(x, norm_mix_g[l])
        proj = h @ w_in[l]
        u_x, u_g, q, k, v, g_lru, g_att = jnp.split(proj, split_idx, axis=-1)

        c = causal_depthwise_conv(u_x, conv_w[l], conv_b[l])
        y_lru = rg_lru(c, w_rgate[l], b_rgate[l], w_igate[l], b_igate[l], lru_lambda[l]) * jax.nn.gelu(u_g)

        to_heads = lambda t: t.reshape(B, S, ATT_HEADS, ATT_HEAD_DIM).transpose(0, 2, 1, 3)
        y_att = stick_breaking_attention(to_heads(q), to_heads(k), to_heads(v))
        y_att = y_att.transpose(0, 2, 1, 3).reshape(B, S, ATT_WIDTH)

        merged = jax.nn.sigmoid(g_lru) * (y_lru @ w_br_lru[l]) + jax.nn.sigmoid(g_att) * (y_att @ w_br_att[l])
        x = x + merged @ w_out[l]

        h2 = rmsnorm(x, norm_mlp_g[l])
        x = x + jnp.square(jax.nn.relu(h2 @ w_mlp_up[l])) @ w_mlp_down[l]

        h3 = rmsnorm(x, norm_ple_g[l])
        x = x + jax.nn.sigmoid(h3 @ w_ple_gate[l]) * (p[l] @ w_ple[l])
    return rmsnorm(x, norm_final_g)
```

```python
import numpy as np
import ml_dtypes
from contextlib import ExitStack
import concourse.bass as bass
import concourse.mybir as mybir
from concourse.bass_utils import run_bass_kernel_spmd

F32 = mybir.dt.float32
BF16 = mybir.dt.bfloat16
AF = mybir.ActivationFunctionType
ALU = mybir.AluOpType

S = 8192
D = 1024
NCORE = 8
EPS = 1e-6
GELU_K = 1.5957691216057308


class Res:
    __slots__ = ("name", "w", "r")

    def __init__(self, name):
        self.name = name
        self.w = None
        self.r = {}


class Prog:
    def __init__(self, nc, es):
        self.nc = nc
        self.es = es
        self.eng = {"pe": nc.tensor, "act": nc.scalar, "dve": nc.vector, "pool": nc.gpsimd, "sp": nc.sync}
        self.stream = {k: [] for k in self.eng}
        self.semh = {}
        self.cnt = {}
        self.known = {k: {} for k in self.eng}
        self.cap = None
        for k in self.eng:
            self._sem("E_" + k)

    def _sem(self, name):
        if name not in self.semh:
            self.semh[name] = self.es.enter_context(self.nc.semaphore(name))
            self.cnt[name] = 0
        return name

    def op(self, eng, fn, reads=(), writes=(), dma=None, cc=False, nowaw=False, extra=None, tokout=None):
        if self.cap is not None:
            self.cap.append((eng, fn, list(reads), list(writes), dma, cc, nowaw, extra, tokout))
            return None
        own = "E_" + eng
        deps = {}

        def add(tok):
            if tok is None:
                return
            n, v = tok
            if deps.get(n, 0) < v:
                deps[n] = v

        for r in reads:
            add(r.w)
        for t_ in (extra or ()):
            add(t_)
        for w in writes:
            if w.w is not None and w.w[0] != own and not nowaw:
                add(w.w)
            for n, v in w.r.items():
                if n != own:
                    add((n, v))
        if cc:
            sn = self._sem("D_" + dma)
            inc = 1
        elif dma is not None:
            sn = self._sem("D_" + dma)
            inc = 16
        else:
            sn = own
            inc = 1
        self.cnt[sn] += inc
        tok = (sn, self.cnt[sn])
        waits = []
        for n, v in deps.items():
            if eng == "pe" and n == own:
                continue
            if self.known[eng].get(n, 0) >= v:
                continue
            self.known[eng][n] = v
            waits.append((n, v))
        self.stream[eng].append((waits, fn, sn, inc))
        for r in reads:
            if r.r.get(sn, 0) < tok[1]:
                r.r[sn] = tok[1]
        for w in writes:
            w.w = tok
            w.r = {}
        if tokout is not None:
            tokout.append(tok)
        return tok

    def barrier(self):
        allc = [(n, v) for n, v in self.cnt.items() if v > 0]
        for k in self.eng:
            waits = []
            for n, v in allc:
                if self.known[k].get(n, 0) >= v:
                    continue
                self.known[k][n] = v
                waits.append((n, v))
            self.stream[k].append((waits, None, None, 0))

    def emit(self):
        with self.nc.Block() as block:
            def mk(k):
                def body(e):
                    for waits, fn, sn, inc in self.stream[k]:
                        for n, v in waits:
                            e.wait_ge(self.semh[n], v)
                        if fn is not None:
                            ins = fn(e)
                            if inc == 1 and sn.startswith("D_"):
                                ins.then_inc(self.semh[sn])
                            else:
                                ins.then_inc(self.semh[sn], inc)
                return body
            block.tensor(mk("pe"))
            block.scalar(mk("act"))
            block.vector(mk("dve"))
            block.gpsimd(mk("pool"))
            block.sync(mk("sp"))


class LazyBank:
    def __init__(self):
        self.ap = None
        self.res = None

    def __getitem__(self, idx):
        return self.ap[idx]


class Ctx:
    FSZ = 17472
    BSZ = 21504
    RSZ = 49152

    def __init__(self, nc):
        self.nc = nc
        self.es = ExitStack()
        self.P = Prog(nc, self.es)
        self.banks = []
        self.ps = self.es.enter_context(nc.psum_tensor("psall", [128, 4096], F32))
        for i in range(8):
            self.banks.append((self.ps[:, 512 * i:512 * (i + 1)], Res("bank%d" % i)))
        self.nb = 0
        self.nbg = 0
        self.pools = {
            "f": [self.es.enter_context(nc.sbuf_tensor("poolF", [128, self.FSZ], F32)), 0, self.FSZ],
            "b": [self.es.enter_context(nc.sbuf_tensor("poolB", [128, self.BSZ], BF16)), 0, self.BSZ],
            "r": [self.es.enter_context(nc.sbuf_tensor("poolR", [128, self.RSZ], BF16)), 0, self.RSZ],
        }
        self.uid = 0
        self.lay = []

    def bank(self, kind="n"):
        if self.P.cap is not None:
            lb = LazyBank()
            lb.kind = kind
            return lb, lb
        b = self.banks[self.nb % 8]
        self.nb += 1
        return b

    def replay(self, pend):
        for eng, fn, reads, writes, dma, cc, nowaw, extra, tokout in pend:
            for lst in (reads, writes):
                for k, r in enumerate(lst):
                    if isinstance(r, LazyBank):
                        if r.ap is None:
                            if getattr(r, "kind", "n") == "g":
                                r.ap, r.res = self.banks[4 + self.nbg % 4]
                                self.nbg += 1
                            else:
                                r.ap, r.res = self.banks[self.nb % 4]
                                self.nb += 1
                        lst[k] = r.res
            self.P.op(eng, fn, reads, writes, dma, cc, nowaw, extra, tokout)

    def sb(self, name, shape, dt, es=None, pool=None):
        if pool is None:
            pool = "f" if dt == F32 else "b"
        pl = self.pools[pool]
        n = 1
        for d in shape[1:]:
            n *= d
        n_al = (n + 1) // 2 * 2
        assert pl[1] + n_al <= pl[2], (name, pool, pl[1], n_al, pl[2])
        ap = pl[0][:, pl[1]:pl[1] + n]
        pl[1] += n_al
        if len(shape) == 3:
            ap = ap.rearrange("p (c t) -> p c t", c=shape[1])
        self.uid += 1
        self.lay.append((name, pool, pl[1] - n_al, n))
        return ap, Res("%s_%d" % (name, self.uid))

    def mark(self):
        self.keep = {k: v[1] for k, v in self.pools.items()}

    def reset(self, *pools):
        for p in pools:
            self.pools[p][1] = self.keep.get(p, 0)


def _rms_tile(C, x_t, x_r, g_t, g_r, ones_t, ones_r, sq, rstd, lnv, hT, TT, eps_t):
    P = C.P
    sq_t, sq_r = sq
    rs_t, rs_r = rstd
    ln_t, ln_r = lnv
    h_t, h_r = hT
    P.op("act", lambda e: e.activation(out=sq_t[:, :, :], in_=x_t[:, :, :], func=AF.Square),
         reads=[x_r], writes=[sq_r])
    bk, bk_r = C.bank()

    def mm(e):
        ins = None
        for c in range(8):
            ins = e.matmul(bk[:, 0:TT], lhsT=ones_t[:, :], rhs=sq_t[:, c, :], start=(c == 0), stop=(c == 7))
        return ins
    P.op("pe", mm, reads=[sq_r, ones_r], writes=[bk_r])
    eps_t, eps_r = eps_t
    P.op("act", lambda e: e.activation(out=ln_t[:, :], in_=bk[:, 0:TT], func=AF.Ln, scale=1.0 / D, bias=eps_t[:, 0:1]),
         reads=[bk_r, eps_r], writes=[ln_r])
    P.op("act", lambda e: e.activation(out=rs_t[:, :], in_=ln_t[:, :], func=AF.Exp, scale=-0.5),
         reads=[ln_r], writes=[rs_r])

    def hh(e):
        ins = None
        for c in range(8):
            ins = e.scalar_tensor_tensor(out=h_t[:, c, :], in0=x_t[:, c, :], scalar=g_t[:, c:c + 1], in1=rs_t[:, :],
                                         op0=ALU.mult, op1=ALU.mult)
        return ins
    P.op("dve", hh, reads=[x_r, g_r, rs_r], writes=[h_r])


def build_phase1(nc, C, dr):
    P = C.P
    es1 = None
    TT = 256
    NT = S // TT

    qT, qT_r = C.sb("qT", [128, 2, S], BF16, pool="r")
    kT, kT_r = C.sb("kT", [128, 2, S], BF16, pool="r")
    vS, vS_r = C.sb("vS", [128, 64, 256], BF16, pool="r")
    ones, ones_r = C.sb("ones", [128, 128], BF16)
    eps_t, eps_r = C.sb("eps", [128, 1], F32)
    one_t, one_r = C.sb("one1", [128, 1], F32)
    cst, cst_r = C.sb("cstb", [128, 2432], BF16)
    C.mark()

    w1b, w1b_r = C.sb("w1b", [128, 8, 1280], BF16, es1)
    wst = [C.sb("wst%d" % i, [128, 1280], F32, es1) for i in range(2)]
    gm, gm_r = C.sb("gm", [128, 8], F32, es1)
    vec, vec_r = C.sb("vec", [128, 16], F32, es1)
    dv, dv_r = C.sb("dv", [128, 8], F32, es1)
    bdst, bdst_r = C.sb("bdst", [128, 4, 128], F32, es1)
    bd, bd_r = C.sb("bd", [128, 4, 128], BF16, es1)

    P.op("pool", lambda e: e.memset(ones[:, :], 1.0), writes=[ones_r])
    P.op("pool", lambda e: e.memset(eps_t[:, :], EPS), writes=[eps_r])
    P.op("pool", lambda e: e.memset(one_t[:, :], 1.0), writes=[one_r])
    P.op("pool", lambda e: e.memset(bdst[:, :, :], 0.0), writes=[bdst_r])
    P.op("sp", lambda e: e.dma_start(out=gm[:, :], in_=dr["gm"][:, :]), writes=[gm_r], dma="gm")
    P.op("sp", lambda e: e.dma_start(out=vec[:, :], in_=dr["vec1"][:, :]), writes=[vec_r], dma="vec")
    for j in range(4):
        ct, hb = j // 2, j % 2
        for nm, base in (("wr", 0), ("wi", 2)):
            P.op("sp", lambda e, nm=nm, j=j, ct=ct, hb=hb, base=base: e.dma_start(
                out=bdst[64 * hb:64 * hb + 64, base + ct, 64 * hb:64 * hb + 64], in_=dr[nm][j, :, :]),
                writes=[bdst_r], dma="bdst")
    P.op("dve", lambda e: e.tensor_copy(out=bd[:, :, :], in_=bdst[:, :, :]), reads=[bdst_r], writes=[bd_r])
    for i in range(2):
        st_t, st_r = wst[i]
        P.op("sp", lambda e, i=i, st_t=st_t: e.dma_start(out=st_t[:, 0:1216], in_=dr["cst"][:, 1216 * i:1216 * (i + 1)]),
             writes=[st_r], dma="wst%d" % i)
        P.op("dve", lambda e, i=i, st_t=st_t: e.tensor_copy(out=cst[:, 1216 * i:1216 * (i + 1)], in_=st_t[:, 0:1216]),
             reads=[st_r], writes=[cst_r])
    for c in range(8):
        st_t, st_r = wst[c % 2]
        P.op("sp", lambda e, c=c, st_t=st_t: e.dma_start(out=st_t[:, :], in_=dr["w1"][128 * c:128 * (c + 1), :]),
             writes=[st_r], dma="wst%d" % (c % 2))
        P.op("pool" if c % 2 else "dve", lambda e, c=c, st_t=st_t: e.tensor_copy(out=w1b[:, c, :], in_=st_t[:, :]),
             reads=[st_r], writes=[w1b_r])
    tmpv, tmpv_r = C.sb("tmpv", [128, 4], F32, es1)
    for ct in range(2):
        P.op("act", lambda e, ct=ct: e.activation(out=tmpv[:, ct:ct + 1], in_=vec[:, 8 * ct + 7:8 * ct + 8], func=AF.Exp, scale=-1.0),
             reads=[vec_r], writes=[tmpv_r])
        P.op("act", lambda e, ct=ct: e.activation(out=tmpv[:, 2 + ct:3 + ct], in_=tmpv[:, ct:ct + 1], func=AF.Ln, bias=one_t[:, 0:1]),
             reads=[tmpv_r, one_r], writes=[tmpv_r])
        P.op("dve", lambda e, ct=ct: e.tensor_scalar(out=dv[:, 4 * ct:4 * ct + 1], in0=tmpv[:, 2 + ct:3 + ct], scalar1=-8.0, scalar2=None, op0=ALU.mult),
             reads=[tmpv_r], writes=[dv_r])
        P.op("dve", lambda e, ct=ct: e.tensor_scalar(out=dv[:, 4 * ct + 1:4 * ct + 2], in0=tmpv[:, 2 + ct:3 + ct], scalar1=-16.0, scalar2=None, op0=ALU.mult),
             reads=[tmpv_r], writes=[dv_r])
        P.op("dve", lambda e, ct=ct: e.tensor_scalar(out=dv[:, 4 * ct + 2:4 * ct + 4], in0=vec[:, 8 * ct + 5:8 * ct + 7], scalar1=-1.0, scalar2=None, op0=ALU.mult),
             reads=[vec_r], writes=[dv_r])

    xs = [C.sb("xs%d" % i, [128, 8, TT], F32, es1) for i in range(2)]
    sq = C.sb("sq", [128, 8, TT], BF16, es1)
    rstd = C.sb("rstd", [128, TT], F32, es1)
    lnv = C.sb("lnv", [128, TT], F32, es1)
    hTs = [C.sb("hT%d" % i, [128, 8, TT], BF16, es1) for i in range(2)]
    uxb = [[C.sb("ux%d_%d" % (ct, i), [128, TT + 4], F32, es1) for i in range(2)] for ct in range(2)]
    ugb = [[C.sb("ug%d_%d" % (ct, i), [128, TT], F32, es1) for i in range(2)] for ct in range(2)]
    T = {}
    for nm in ("c", "r", "i", "a", "a2", "m", "u", "w", "e", "gg"):
        for ct in range(2):
            T[nm, ct] = C.sb("t_%s%d" % (nm, ct), [128, TT], F32, es1)
    rscr = C.sb("rscr", [128, TT], F32, es1)
    cbf = [C.sb("cbf%d" % ct, [128, TT], BF16, es1) for ct in range(2)]
    hh = [[C.sb("h%d_%d" % (ct, i), [128, TT], F32, es1) for i in range(2)] for ct in range(2)]
    ysb = [[C.sb("y%d_%d" % (ct, i), [128, TT], BF16, es1) for i in range(2)] for ct in range(2)]
    for ct in range(2):
        P.op("pool", lambda e, ct=ct: e.memset(uxb[ct][0][0][:, 0:3], 0.0), writes=[uxb[ct][0][1]])

    xTv = dr["xT"].rearrange("(c p) t -> p c t", p=128)

    def load_x(tt):
        x_t, x_r = xs[tt % 2]
        P.op("sp", lambda e: e.dma_start(out=x_t[:, :, :], in_=xTv[:, :, tt * TT:(tt + 1) * TT]),
             writes=[x_r], dma="xs%d" % (tt % 2))

    def tile(tt):
        t0 = tt * TT
        if tt + 1 < NT:
            load_x(tt + 1)
        x_t, x_r = xs[tt % 2]
        h_t, h_r = hTs[tt % 2]
        ux = [uxb[ct][tt % 2] for ct in range(2)]
        ug = [ugb[ct][tt % 2] for ct in range(2)]
        _rms_tile(C, x_t, x_r, gm, gm_r, ones, ones_r, sq, rstd, lnv, (h_t, h_r), TT, (eps_t, eps_r))
        if P.cap is not None:
            split.append(len(P.cap))
        for n in range(8):
            bk, bk_r = C.bank()

            def mm(e, n=n, bk=bk):
                ins = None
                for c in range(8):
                    ins = e.matmul(bk[:, 0:TT], lhsT=w1b[:, c, 128 * n:128 * (n + 1)], rhs=h_t[:, c, :],
                                   start=(c == 0), stop=(c == 7))
                return ins
            P.op("pe", mm, reads=[w1b_r, h_r], writes=[bk_r])
            if n < 2:
                P.op("dve", lambda e, n=n, bk=bk: e.tensor_copy(out=ux[n][0][:, 3:3 + TT], in_=bk[:, 0:TT]),
                     reads=[bk_r], writes=[ux[n][1]])
                if tt > 0:
                    up_t, up_r = uxb[n][(tt + 1) % 2]
                    P.op("pool", lambda e, n=n, up_t=up_t: e.tensor_copy(out=ux[n][0][:, 0:3], in_=up_t[:, TT:TT + 3]),
                         reads=[up_r], writes=[ux[n][1]])
            elif n < 4:
                P.op("dve", lambda e, n=n, bk=bk: e.tensor_copy(out=ug[n - 2][0][:, :], in_=bk[:, 0:TT]),
                     reads=[bk_r], writes=[ug[n - 2][1]])
            elif n < 6:
                P.op("dve", lambda e, n=n, bk=bk: e.tensor_scalar(out=qT[:, n - 4, t0:t0 + TT], in0=bk[:, 0:TT], scalar1=0.125, scalar2=None, op0=ALU.mult),
                     reads=[bk_r], writes=[qT_r])
            else:
                P.op("dve", lambda e, n=n, bk=bk: e.tensor_copy(out=kT[:, n - 6, t0:t0 + TT], in_=bk[:, 0:TT]),
                     reads=[bk_r], writes=[kT_r])
        for sub in range(TT // 128):
            bk, bk_r = C.bank()
            kb = (t0 // 128) + sub

            def mmv(e, sub=sub, bk=bk):
                ins = None
                for c in range(8):
                    ins = e.matmul(bk[:, 0:256], lhsT=h_t[:, c, 128 * sub:128 * (sub + 1)], rhs=w1b[:, c, 1024:1280],
                                   start=(c == 0), stop=(c == 7))
                return ins
            P.op("pe", mmv, reads=[w1b_r, h_r], writes=[bk_r])
            P.op("dve", lambda e, kb=kb, bk=bk: e.tensor_copy(out=vS[:, kb, :], in_=bk[:, 0:256]),
                 reads=[bk_r], writes=[vS_r])
        if P.cap is not None:
            split.append(len(P.cap))
        def TT_(nm, ct):
            return T[nm, ct]
        for k in (3, 2, 1, 0):
            for ct in range(2):
                c_t, c_r = T["c", ct]
                u_t, u_r = ux[ct]
                if k == 3:
                    P.op("dve", lambda e, ct=ct, c_t=c_t, u_t=u_t: e.tensor_scalar(
                        out=c_t[:, :], in0=u_t[:, 3:3 + TT], scalar1=vec[:, 8 * ct + 3:8 * ct + 4], scalar2=vec[:, 8 * ct + 4:8 * ct + 5],
                        op0=ALU.mult, op1=ALU.add), reads=[u_r, vec_r], writes=[c_r])
                else:
                    P.op("dve", lambda e, ct=ct, k=k, c_t=c_t, u_t=u_t: e.scalar_tensor_tensor(
                        out=c_t[:, :], in0=u_t[:, k:k + TT], scalar=vec[:, 8 * ct + k:8 * ct + k + 1], in1=c_t[:, :],
                        op0=ALU.mult, op1=ALU.add), reads=[u_r, vec_r, c_r], writes=[c_r])
        for ct in range(2):
            u_t, u_r = ux[ct]
            P.op("pool", lambda e, ct=ct: e.tensor_copy(out=cbf[ct][0][:, :], in_=T["c", ct][0][:, :]),
                 reads=[T["c", ct][1]], writes=[cbf[ct][1]])
        gates = []
        gbanks = []
        for ct in range(2):
            for gi, nm in ((0, "r"), (1, "i")):
                bk, bk_r = C.bank(kind="g")
                P.op("pe", lambda e, ct=ct, gi=gi, bk=bk: e.matmul(bk[:, 0:TT], lhsT=bd[:, 2 * gi + ct, :], rhs=cbf[ct][0][:, :], start=True, stop=True),
                     reads=[bd_r, cbf[ct][1]], writes=[bk_r])
                gbanks.append((ct, gi, nm, bk, bk_r))
        if P.cap is not None:
            split.append(len(P.cap))
        for ct, gi, nm, bk, bk_r in gbanks:
            o_t, o_r = T[nm, ct]
            P.op("act", lambda e, ct=ct, gi=gi, bk=bk, o_t=o_t: e.activation(
                out=o_t[:, :], in_=bk[:, 0:TT], func=AF.Exp, scale=-1.0, bias=dv[:, 4 * ct + 2 + gi:4 * ct + 3 + gi]),
                reads=[bk_r, dv_r], writes=[o_r])
            gates.append((o_t, o_r))
        for o_t, o_r in gates:
            P.op("act", lambda e, o_t=o_t: e.activation(out=o_t[:, :], in_=o_t[:, :], func=AF.Ln, bias=1.0),
                 reads=[o_r], writes=[o_r])
        for o_t, o_r in gates:
            P.op("act", lambda e, o_t=o_t: e.activation(out=o_t[:, :], in_=o_t[:, :], func=AF.Exp, scale=-1.0),
                 reads=[o_r], writes=[o_r])
        for ct in range(2):
            r_t, r_r = T["r", ct]
            P.op("act", lambda e, ct=ct, r_t=r_t: e.activation(out=T["a", ct][0][:, :], in_=r_t[:, :], func=AF.Exp, scale=dv[:, 4 * ct:4 * ct + 1]),
                 reads=[r_r, dv_r], writes=[T["a", ct][1]])
            P.op("act", lambda e, ct=ct, r_t=r_t: e.activation(out=T["a2", ct][0][:, :], in_=r_t[:, :], func=AF.Exp, scale=dv[:, 4 * ct + 1:4 * ct + 2]),
                 reads=[r_r, dv_r], writes=[T["a2", ct][1]])
        for ct in range(2):
            P.op("dve", lambda e, ct=ct: e.tensor_scalar(out=T["a2", ct][0][:, :], in0=T["a2", ct][0][:, :], scalar1=-1.0, scalar2=1.0, op0=ALU.mult, op1=ALU.add),
                 reads=[T["a2", ct][1]], writes=[T["a2", ct][1]])
        for ct in range(2):
            P.op("act", lambda e, ct=ct: e.activation(out=T["m", ct][0][:, :], in_=T["a2", ct][0][:, :], func=AF.Ln),
                 reads=[T["a2", ct][1]], writes=[T["m", ct][1]])
        for ct in range(2):
            P.op("act", lambda e, ct=ct: e.activation(out=T["m", ct][0][:, :], in_=T["m", ct][0][:, :], func=AF.Exp, scale=0.5),
                 reads=[T["m", ct][1]], writes=[T["m", ct][1]])
        for ct in range(2):
            g_t, g_r = ug[ct]
            w_t, w_r = T["w", ct]
            P.op("pool", lambda e, g_t=g_t, w_t=w_t: e.tensor_tensor(out=w_t[:, :], in0=g_t[:, :], in1=g_t[:, :], op=ALU.mult),
                 reads=[g_r], writes=[w_r])
        for ct in range(2):
            w_t, w_r = T["w", ct]
            P.op("pool", lambda e, w_t=w_t: e.tensor_scalar(out=w_t[:, :], in0=w_t[:, :], scalar1=0.044715, scalar2=1.0, op0=ALU.mult, op1=ALU.add),
                 reads=[w_r], writes=[w_r])
        for ct in range(2):
            g_t, g_r = ug[ct]
            w_t, w_r = T["w", ct]
            P.op("pool", lambda e, g_t=g_t, w_t=w_t: e.tensor_tensor(out=w_t[:, :], in0=w_t[:, :], in1=g_t[:, :], op=ALU.mult),
                 reads=[w_r, g_r], writes=[w_r])
        for ct in range(2):
            P.op("act", lambda e, ct=ct: e.activation(out=T["e", ct][0][:, :], in_=T["w", ct][0][:, :], func=AF.Exp, scale=-GELU_K),
                 reads=[T["w", ct][1]], writes=[T["e", ct][1]])
        for ct in range(2):
            P.op("pool", lambda e, ct=ct: e.tensor_tensor(out=T["u", ct][0][:, :], in0=T["m", ct][0][:, :], in1=T["i", ct][0][:, :], op=ALU.mult),
                 reads=[T["m", ct][1], T["i", ct][1]], writes=[T["u", ct][1]])
        for ct in range(2):
            P.op("pool", lambda e, ct=ct: e.tensor_tensor(out=T["u", ct][0][:, :], in0=T["u", ct][0][:, :], in1=T["c", ct][0][:, :], op=ALU.mult),
                 reads=[T["u", ct][1], T["c", ct][1]], writes=[T["u", ct][1]])
        if P.cap is not None:
            split.append(len(P.cap))
        for ct in range(2):
            hc_t, hc_r = hh[ct][tt % 2]
            hp_t, hp_r = hh[ct][(tt + 1) % 2]
            if tt == 0:
                P.op("dve", lambda e, ct=ct, hc_t=hc_t: e.tensor_tensor_scan(
                    out=hc_t[:, :], data0=T["a", ct][0][:, :], data1=T["u", ct][0][:, :], initial=0.0, op0=ALU.mult, op1=ALU.add),
                    reads=[T["a", ct][1], T["u", ct][1]], writes=[hc_r])
            else:
                P.op("dve", lambda e, ct=ct, hc_t=hc_t, hp_t=hp_t: e.tensor_tensor_scan(
                    out=hc_t[:, :], data0=T["a", ct][0][:, :], data1=T["u", ct][0][:, :], initial=hp_t[:, TT - 1:TT], op0=ALU.mult, op1=ALU.add),
                    reads=[T["a", ct][1], T["u", ct][1], hp_r], writes=[hc_r])
        for ct in range(2):
            P.op("act", lambda e, ct=ct: e.activation(out=T["e", ct][0][:, :], in_=T["e", ct][0][:, :], func=AF.Ln, bias=1.0),
                 reads=[T["e", ct][1]], writes=[T["e", ct][1]])
        for ct in range(2):
            P.op("act", lambda e, ct=ct: e.activation(out=T["e", ct][0][:, :], in_=T["e", ct][0][:, :], func=AF.Exp, scale=-1.0),
                 reads=[T["e", ct][1]], writes=[T["e", ct][1]])
        for ct in range(2):
            P.op("pool", lambda e, ct=ct: e.tensor_tensor(out=T["gg", ct][0][:, :], in0=T["e", ct][0][:, :], in1=ug[ct][0][:, :], op=ALU.mult),
                 reads=[T["e", ct][1], ug[ct][1]], writes=[T["gg", ct][1]])
        for ct in range(2):
            y_t, y_r = ysb[ct][tt % 2]
            hc_t, hc_r = hh[ct][tt % 2]
            P.op("pool", lambda e, ct=ct, y_t=y_t, hc_t=hc_t: e.tensor_tensor(out=y_t[:, :], in0=T["gg", ct][0][:, :], in1=hc_t[:, :], op=ALU.mult),
                 reads=[T["gg", ct][1], hc_r], writes=[y_r])
            jj, co = t0 // 1024, t0 % 1024
            P.op("sp", lambda e, ct=ct, y_t=y_t, co=co, jj=jj: e.dma_start(out=dr["yin"][jj][128 * ct:128 * (ct + 1), co:co + TT], in_=y_t[:, :]),
                 reads=[y_r], dma="y%d_%d" % (ct, tt % 2), tokout=dr["yin_tok"][jj])

    pcl = []
    if "wsc_r" in dr:
        for nm, K_, N_ in WSPEC:
            for c in range(K_ // 128):
                pcl.append((nm, c))

    def precast(i):
        nm, c = pcl[i]
        P.op("pool", lambda e: e.dma_start(out=dr[nm + "_b"][128 * c:128 * (c + 1), :], in_=dr[nm][128 * c:128 * (c + 1), :],
                                           max_dma_last_dim=4096),
             writes=[dr["wsc_r"][nm]], dma="pc_" + nm, nowaw=True)

    split = []
    load_x(0)
    npc = 0

    def cap_tile(tt):
        P.cap = []
        tile(tt)
        ops, P.cap = P.cap, None
        m4 = split.pop()
        m3 = split.pop()
        m2 = split.pop()
        m1 = split.pop()
        return ops[:m1], ops[m1:m2], ops[m2:m3], ops[m3:m4], ops[m4:]

    def mix(a, b):
        out = []
        ia = ib = 0
        na, nb_ = len(a), len(b)
        while ia < na or ib < nb_:
            if ib >= nb_ or (ia < na and ia * nb_ <= ib * na):
                out.append(a[ia])
                ia += 1
            else:
                out.append(b[ib])
                ib += 1
        return out

    parts = {}
    parts[0] = cap_tile(0)
    C.replay(parts[0][0])
    C.replay(parts[0][1])
    C.replay(parts[0][2])
    parts[1] = cap_tile(1)
    C.replay(parts[1][0])
    for tt in range(NT):
        if tt + 2 < NT:
            parts[tt + 2] = cap_tile(tt + 2)
            C.replay(parts[tt + 2][0])
        nxtB = parts[tt + 1][1] if tt + 1 < NT else []
        C.replay(mix(nxtB, parts[tt][3]))
        if tt + 1 < NT:
            C.replay(parts[tt + 1][2])
        C.replay(parts[tt][4])
        del parts[tt]
    dr["precast_fn"] = precast
    dr["precast_n"] = len(pcl)
    P.barrier()
    if "dbgF" in dr:
        C.layout = {"vec": 0}
        P.op("sp", lambda e: e.dma_start(out=dr["dbgF"][:, :], in_=C.pools["f"][0][:, :]), dma="dbgF")
        P.op("sp", lambda e: e.dma_start(out=dr["dbgB"][:, :], in_=C.pools["b"][0][:, :]), dma="dbgB")
        P.op("sp", lambda e: e.dma_start(out=dr["dbgR"][:, :], in_=C.pools["r"][0][:, :]), dma="dbgR")
        P.barrier()
        return
    C.reset("f", "b")
    build_attention(nc, C, dr, qT, qT_r, kT, kT_r, vS, vS_r, cst, cst_r, one_t, one_r)


def build_attention(nc, C, dr, qT, qT_r, kT, kT_r, vS, vS_r, cst, cst_r, one_t, one_r):
    P = C.P
    ZB = [(C.ps[:, 1024 * k:1024 * (k + 1)], Res("zb%d" % k)) for k in range(3)]
    O_b = [(C.ps[:, 3072 + 512 * h:3072 + 512 * (h + 1)], Res("ob%d" % h)) for h in range(2)]
    E_s = [C.sb("E%d" % i, [128, 1024], F32) for i in range(2)]
    L_s = [C.sb("Lp%d" % i, [128, 1024], BF16) for i in range(3)]
    A_s = [C.sb("As%d" % i, [128, 1024], BF16) for i in range(3)]
    Sacc = C.sb("Sacc", [128, 1024], F32)
    Sbf = [C.sb("Sbf%d" % j, [128, 1024], BF16) for j in range(2)]
    yst = [C.sb("yst%d" % i, [128, 512], BF16) for i in range(4)]
    triN = cst[:, 0:128]
    onesN = cst[:, 128:256]

    ident = cst[:, 256:384]

    def negmask(j):
        return cst[:, 384 + 512 * j:384 + 512 * (j + 1)]

    steps = []
    for qt in range(S // 512):
        for pair in range(2):
            for kb in range(4 * qt + 3, -1, -1):
                steps.append((pair, qt, kb))
    NB = len(steps)
    nyst = [0]
    sprev = {}
    ns = 0
    for i, (pair, qt, kb) in enumerate(steps):
        first = (kb == 4 * qt + 3)
        second = (kb == 4 * qt + 2)
        if second:
            sprev[i] = L_s[(i - 1) % 3]
        elif not first:
            sprev[i] = Sbf[(ns - 1) % 2]
        if (not first) and kb > 0:
            ns += 1
    nsw = [0]

    def pe1(i):
        pair, qt, kb = steps[i]
        z_t, z_r = ZB[i % 3]

        jd = kb - 4 * qt

        def f(e):
            ins = None
            for hs in range(2):
                lo = 64 * hs
                ins = e.matmul(z_t[:, 512 * hs:512 * (hs + 1)], lhsT=kT[lo:lo + 64, pair, 128 * kb:128 * (kb + 1)],
                               rhs=qT[lo:lo + 64, pair, 512 * qt:512 * (qt + 1)], start=True, stop=False, skip_group_check=True)
                if jd >= 0:
                    ins = e.matmul(z_t[:, 512 * hs:512 * (hs + 1)], lhsT=ident, rhs=negmask(jd), start=False, stop=False, skip_group_check=True)
            return ins
        P.op("pe", f, reads=[kT_r, qT_r, cst_r], writes=[z_r])

    def act12(i):
        pair, qt, kb = steps[i]
        z_t, z_r = ZB[i % 3]
        e_t, e_r = E_s[i % 2]
        l_t, l_r = L_s[i % 3]
        P.op("act", lambda e: e.activation(out=e_t[:, :], in_=z_t[:, :], func=AF.Exp), reads=[z_r], writes=[e_r])
        P.op("act", lambda e: e.activation(out=l_t[:, :], in_=e_t[:, :], func=AF.Ln, bias=1.0), reads=[e_r], writes=[l_r])
        first = (kb == 4 * qt + 3)
        if kb > 0:
            s_t, s_r = Sacc
            if first:
                P.op("dve", lambda e: e.tensor_copy(out=s_t[:, :], in_=l_t[:, :]), reads=[l_r], writes=[s_r])
            else:
                P.op("dve", lambda e: e.tensor_tensor(out=s_t[:, :], in0=s_t[:, :], in1=l_t[:, :], op=ALU.add),
                     reads=[s_r, l_r], writes=[s_r])
                sb_t, sb_r = Sbf[nsw[0] % 2]
                nsw[0] += 1
                P.op("dve", lambda e: e.tensor_copy(out=sb_t[:, :], in_=s_t[:, :]), reads=[s_r], writes=[sb_r])

    def pe2(i):
        pair, qt, kb = steps[i]
        z_t, z_r = ZB[i % 3]
        l_t, l_r = L_s[i % 3]
        first = (kb == 4 * qt + 3)
        rd = [l_r, cst_r, z_r]
        if not first:
            sb_t, sb_r = sprev[i]
            rd.append(sb_r)

        def f(e):
            ins = None
            for hs in range(2):
                cs = slice(512 * hs, 512 * (hs + 1))
                ins = e.matmul(z_t[:, cs], lhsT=triN, rhs=l_t[:, cs], start=False, stop=first, skip_group_check=True)
                if not first:
                    ins = e.matmul(z_t[:, cs], lhsT=onesN, rhs=sb_t[:, cs], start=False, stop=True, skip_group_check=True)
            return ins
        P.op("pe", f, reads=rd, writes=[z_r])

    def act3(i):
        pair, qt, kb = steps[i]
        z_t, z_r = ZB[i % 3]
        as_t, as_r = A_s[i % 3]
        P.op("act", lambda e: e.activation(out=as_t[:, :], in_=z_t[:, :], func=AF.Exp), reads=[z_r], writes=[as_r])

    def pe3(i):
        pair, qt, kb = steps[i]
        as_t, as_r = A_s[i % 3]
        first = (kb == 4 * qt + 3)
        last = (kb == 0)

        def f(e):
            ins = None
            for hs in range(2):
                head = 2 * pair + hs
                ins = e.matmul(O_b[hs][0][0:64, :], lhsT=vS[:, kb, 64 * head:64 * (head + 1)], rhs=as_t[:, 512 * hs:512 * (hs + 1)],
                               start=first, stop=last)
            return ins
        P.op("pe", f, reads=[vS_r, as_r], writes=[O_b[0][1], O_b[1][1]])
        if last:
            jj, co = qt // 2, 512 * (qt % 2)
            for hs in range(2):
                head = 2 * pair + hs
                y_t, y_r = yst[nyst[0] % 4]
                nyst[0] += 1
                P.op("dve", lambda e, hs=hs, y_t=y_t: e.tensor_copy(out=y_t[0:64, :], in_=O_b[hs][0][0:64, :]), reads=[O_b[hs][1]], writes=[y_r])
                P.op("sp", lambda e, head=head, y_t=y_t: e.dma_start(
                    out=dr["yin"][jj][256 + 64 * head:256 + 64 * (head + 1), co:co + 512], in_=y_t[0:64, :]),
                    reads=[y_r], dma=y_r.name, tokout=dr["yin_tok"][jj])
            if pair == 1 and qt % 2 == 1:
                dr["issue_cc"](jj)

    npcs = dr.get("precast_n", 0)
    pcdone = 0
    for j in range(NB + 3):
        tgt = min(npcs, (npcs * (j + 1) * 5) // (NB * 4) + 1) if npcs else 0
        while pcdone < tgt:
            dr["precast_fn"](pcdone)
            pcdone += 1
        if j < NB:
            pe1(j)
        if 0 <= j - 2 < NB:
            pe2(j - 2)
        if 0 <= j - 3 < NB:
            pe3(j - 3)
        if 0 <= j - 1 < NB:
            act12(j - 1)
        if 0 <= j - 2 < NB:
            act3(j - 2)
    P.barrier()


WSPEC = (("wg", 1024, 2048), ("wbl", 1024, 1024), ("wba", 1024, 1024), ("wo", 1024, 1024), ("wup", 1024, 4096),
         ("wdn", 4096, 1024), ("wpg", 1024, 1024), ("wple", 256, 1024))


def build_phase2(nc, C, dr):
    P = C.P
    TT = 1024
    NS = TT // 512
    NT2 = 2048 // TT
    C.pools["f"][1] = 0
    C.pools["b"][1] = 0
    C.pools["r"][1] = 0
    ones, ones_r = C.sb("ones2", [128, 128], BF16)
    eps_t, eps_r = C.sb("eps2", [128, 1], F32)
    g2, g2_r = C.sb("g2", [128, 32], F32)
    P.op("pool", lambda e: e.memset(ones[:, :], 1.0), writes=[ones_r])
    P.op("pool", lambda e: e.memset(eps_t[:, :], EPS), writes=[eps_r])
    P.op("sp", lambda e: e.dma_start(out=g2[:, :], in_=dr["g2"][:, :]), writes=[g2_r], dma="g2")
    x_t, x_r = C.sb("x2", [128, 8, TT], F32)
    rstd = C.sb("rstd2", [128, 512], F32)
    lnv = C.sb("lnv2", [128, 512], F32)
    tA = C.sb("tA", [128, 4 * NS, 512], F32)
    tB = C.sb("tB", [128, 4 * NS, 512], F32)
    pst = (tA[0][:, 0:4, :].rearrange("p (c a) n -> p c (a n)", c=2), tA[1])
    hT = C.sb("h2T", [128, 8, TT], BF16)
    mg = C.sb("mg", [128, 8, TT], BF16)
    sq = mg
    pb = C.sb("pb", [128, 2, TT], BF16)
    act = C.sb("actT", [128, 32, TT], BF16, pool="r")
    yt = (act[0][:, 0:16, :], act[1])
    wb = [C.sb("wb%d" % i, [128, 8, 512], BF16, pool="r") for i in range(3)]
    wpl = C.sb("wpl", [128, 2, 1024], BF16, pool="r")
    ng = [0]

    def gran(wname, r0, c0, kc=8, gc=512, dst=None):
        i = ng[0]
        ng[0] += 1
        w_t, w_r = dst if dst is not None else wb[i % 3]
        src = dr[wname + "_b"][r0:r0 + 128 * kc, c0:c0 + gc].rearrange("(c p) n -> p c n", p=128)
        wv = w_t if dst is not None else w_t[:, :, 0:gc]
        P.op("sp", lambda e: e.dma_start(out=wv, in_=src), reads=[dr["wsc_r"][wname]], writes=[w_r], dma=w_r.name)
        return w_t, w_r

    def mmgroup(bk, bk_r, w_t, w_r, col, rhs_list, rhs_res, start=True, stop=True):
        n = len(rhs_list)

        def f(e):
            ins = None
            for c in range(n):
                ins = e.matmul(bk[:, :], lhsT=w_t[:, c, 128 * col:128 * (col + 1)], rhs=rhs_list[c],
                               start=(start and c == 0), stop=(stop and c == n - 1))
            return ins
        P.op("pe", f, reads=[w_r] + list(rhs_res), writes=[bk_r])

    xv = dr["xT2"].rearrange("(c p) t -> p c t", p=128)
    ov = dr["oT"].rearrange("(c p) t -> p c t", p=128)
    pv = dr["pT"].rearrange("(c p) t -> p c t", p=128)

    def sl(sub):
        return slice(512 * sub, 512 * (sub + 1))

    def tile2(tt):
        t0 = tt * TT
        y_t, y_r = yt
        P.op("sp", lambda e: e.dma_start(out=x_t[:, :, :], in_=xv[:, :, t0:t0 + TT]), writes=[x_r], dma=x_r.name)
        for r in range(4):
            def ld(e, r=r):
                if "rank" not in dr:
                    dr["rank"] = e.partition_id() % 4
                rank = dr["rank"]
                src = dr["yall"][bass.ds((rank * 2 + tt) * 2048 + 512 * r, 512), :].rearrange("(c p) t -> p c t", p=128)
                return e.dma_start(out=y_t[:, 4 * r:4 * r + 4, :], in_=src)
            P.op("pool", ld, reads=list(dr["yall_r"]), writes=[y_r], dma=y_r.name)
        P.op("sp", lambda e: e.dma_start(out=pst[0][:, :, :], in_=pv[:, :, t0:t0 + TT]), writes=[pst[1]], dma=pst[1].name)
        P.op("pool", lambda e: e.tensor_copy(out=pb[0][:, :, :], in_=pst[0][:, :, :]), reads=[pst[1]], writes=[pb[1]])

        def ylru(sub):
            return [y_t[:, 4 * (c // 2) + (c % 2), sl(sub)] for c in range(8)]

        def yatt(sub):
            return [y_t[:, 4 * (c // 2) + 2 + (c % 2), sl(sub)] for c in range(8)]

        def hl(sub):
            return [hT[0][:, c, sl(sub)] for c in range(8)]

        def rms(gcol, out):
            for sub in range(NS):
                _rms_tile(C, x_t[:, :, sl(sub)], x_r, g2[:, gcol:gcol + 8], g2_r, ones, ones_r,
                          (sq[0][:, :, sl(sub)], sq[1]), rstd, lnv,
                          (out[0][:, :, sl(sub)], out[1]), 512, (eps_t, eps_r))

        rms(0, hT)
        for cg in range(2):
            w_t, w_r = gran("wg", 0, 512 * cg)
            for t in range(4):
                for sub in range(NS):
                    bk, bk_r = C.bank()
                    mmgroup(bk, bk_r, w_t, w_r, t, hl(sub), [hT[1]])
                    P.op("act", lambda e, t=t, sub=sub, bk=bk: e.activation(out=tA[0][:, NS * t + sub, :], in_=bk[:, :], func=AF.Sigmoid),
                         reads=[bk_r], writes=[tA[1]])
            w_t, w_r = gran("wbl", 0, 512 * cg)
            for t in range(4):
                for sub in range(NS):
                    bk, bk_r = C.bank()
                    mmgroup(bk, bk_r, w_t, w_r, t, ylru(sub), [y_r])
                    P.op("dve", lambda e, t=t, sub=sub, bk=bk: e.tensor_tensor(out=tA[0][:, NS * t + sub, :], in0=tA[0][:, NS * t + sub, :], in1=bk[:, :], op=ALU.mult),
                         reads=[bk_r, tA[1]], writes=[tA[1]])
            w_t, w_r = gran("wg", 0, 1024 + 512 * cg)
            for t in range(4):
                for sub in range(NS):
                    bk, bk_r = C.bank()
                    mmgroup(bk, bk_r, w_t, w_r, t, hl(sub), [hT[1]])
                    P.op("act", lambda e, t=t, sub=sub, bk=bk: e.activation(out=tB[0][:, NS * t + sub, :], in_=bk[:, :], func=AF.Sigmoid),
                         reads=[bk_r], writes=[tB[1]])
            w_t, w_r = gran("wba", 0, 512 * cg)
            for t in range(4):
                for sub in range(NS):
                    bk, bk_r = C.bank()
                    mmgroup(bk, bk_r, w_t, w_r, t, yatt(sub), [y_r])
                    P.op("dve", lambda e, t=t, sub=sub, bk=bk: e.tensor_tensor(out=tB[0][:, NS * t + sub, :], in0=tB[0][:, NS * t + sub, :], in1=bk[:, :], op=ALU.mult),
                         reads=[bk_r, tB[1]], writes=[tB[1]])
            for t in range(4):
                P.op("pool", lambda e, cg=cg, t=t: e.tensor_tensor(
                    out=mg[0][:, 4 * cg + t, :], in0=tA[0][:, NS * t:NS * (t + 1), :].rearrange("p s n -> p (s n)"),
                    in1=tB[0][:, NS * t:NS * (t + 1), :].rearrange("p s n -> p (s n)"), op=ALU.add),
                    reads=[tA[1], tB[1]], writes=[mg[1]])
        for cg in range(2):
            w_t, w_r = gran("wo", 0, 512 * cg)
            for t in range(4):
                for sub in range(NS):
                    bk, bk_r = C.bank()
                    mmgroup(bk, bk_r, w_t, w_r, t, [mg[0][:, c, sl(sub)] for c in range(8)], [mg[1]])
                    P.op("dve", lambda e, t=t, cg=cg, sub=sub, bk=bk: e.tensor_tensor(
                        out=x_t[:, 4 * cg + t, sl(sub)], in0=x_t[:, 4 * cg + t, sl(sub)], in1=bk[:, :], op=ALU.add),
                        reads=[bk_r, x_r], writes=[x_r])
        rms(8, hT)
        for cg in range(8):
            w_t, w_r = gran("wup", 0, 512 * cg)
            for t in range(4):
                for sub in range(NS):
                    bk, bk_r = C.bank()
                    mmgroup(bk, bk_r, w_t, w_r, t, hl(sub), [hT[1]])
                    tmp = tA if (t + sub) % 2 == 0 else tB
                    P.op("act", lambda e, bk=bk, tmp=tmp, t=t, sub=sub: e.activation(out=tmp[0][:, NS * t + sub, :], in_=bk[:, :], func=AF.Relu),
                         reads=[bk_r], writes=[tmp[1]])
                    P.op("pool" if (t + sub) % 2 else "dve", lambda e, t=t, cg=cg, tmp=tmp, sub=sub: e.tensor_tensor(
                        out=act[0][:, 4 * cg + t, sl(sub)], in0=tmp[0][:, NS * t + sub, :], in1=tmp[0][:, NS * t + sub, :], op=ALU.mult),
                        reads=[tmp[1]], writes=[act[1]])
        for cg in range(4):
            bks = [[C.bank() for sub in range(NS)] for t in range(2)]
            for kg in range(4):
                w_t, w_r = gran("wdn", 1024 * kg, 256 * cg, gc=256)
                for t in range(2):
                    for sub in range(NS):
                        al = [act[0][:, 8 * kg + c, sl(sub)] for c in range(8)]
                        mmgroup(bks[t][sub][0], bks[t][sub][1], w_t, w_r, t, al, [act[1]], start=(kg == 0), stop=(kg == 3))
            for t in range(2):
                for sub in range(NS):
                    P.op("dve", lambda e, t=t, cg=cg, sub=sub, bk=bks[t][sub][0]: e.tensor_tensor(
                        out=x_t[:, 2 * cg + t, sl(sub)], in0=x_t[:, 2 * cg + t, sl(sub)], in1=bk[:, :], op=ALU.add),
                        reads=[bks[t][sub][1], x_r], writes=[x_r])
        rms(16, hT)
        wp_t, wp_r = gran("wple", 0, 0, kc=2, gc=1024, dst=wpl)
        for cg in range(2):
            w_t, w_r = gran("wpg", 0, 512 * cg)
            for t in range(4):
                for sub in range(NS):
                    bk, bk_r = C.bank()
                    mmgroup(bk, bk_r, w_t, w_r, t, hl(sub), [hT[1]])
                    P.op("act", lambda e, t=t, sub=sub, bk=bk: e.activation(out=tA[0][:, NS * t + sub, :], in_=bk[:, :], func=AF.Sigmoid),
                         reads=[bk_r], writes=[tA[1]])
            for t in range(4):
                for sub in range(NS):
                    bk, bk_r = C.bank()
                    mmgroup(bk, bk_r, wp_t, wp_r, 4 * cg + t, [pb[0][:, c, sl(sub)] for c in range(2)], [pb[1]])
                    P.op("dve", lambda e, t=t, sub=sub, bk=bk: e.tensor_tensor(out=tA[0][:, NS * t + sub, :], in0=tA[0][:, NS * t + sub, :], in1=bk[:, :], op=ALU.mult),
                         reads=[bk_r, tA[1]], writes=[tA[1]])
            for t in range(4):
                P.op("dve", lambda e, cg=cg, t=t: e.tensor_tensor(
                    out=x_t[:, 4 * cg + t, :], in0=x_t[:, 4 * cg + t, :], in1=tA[0][:, NS * t:NS * (t + 1), :].rearrange("p s n -> p (s n)"), op=ALU.add),
                    reads=[tA[1], x_r], writes=[x_r])
        rms(24, (x_t, x_r))
        P.op("sp", lambda e: e.dma_start(out=ov[:, :, t0:t0 + TT], in_=x_t[:, :, :]), reads=[x_r], dma="outst")

    for tt in range(NT2):
        tile2(tt)
    P.barrier()


def _consts():
    c = np.zeros((128, 2432), np.float32)
    jj = np.arange(128)[:, None]
    ss = np.arange(128)[None, :]
    c[:, 0:128] = -(jj >= ss).astype(np.float32)
    c[:, 128:256] = -1.0
    c[:, 256:384] = np.eye(128, dtype=np.float32)
    cc = np.arange(512)[None, :]
    for j in range(4):
        c[:, 384 + 512 * j:384 + 512 * (j + 1)] = -30000.0 * ((128 * j + jj) >= cc).astype(np.float32)
    return c


def _p1_inputs(inp, b, g):
    w_in = inp["w_in"][0]
    sl = slice(256 * g, 256 * (g + 1))
    cols = [w_in[:, 0 * 1024:1 * 1024][:, sl], w_in[:, 1 * 1024:2 * 1024][:, sl], w_in[:, 2 * 1024:3 * 1024][:, sl],
            w_in[:, 3 * 1024:4 * 1024][:, sl], w_in[:, 4 * 1024:5 * 1024][:, sl]]
    w1 = np.ascontiguousarray(np.concatenate(cols, axis=1))
    vec = np.zeros((128, 16), np.float32)
    for ct in range(2):
        ch = slice(256 * g + 128 * ct, 256 * g + 128 * (ct + 1))
        for k in range(4):
            vec[:, 8 * ct + k] = inp["conv_w"][0, k, ch]
        vec[:, 8 * ct + 4] = inp["conv_b"][0, ch]
        vec[:, 8 * ct + 5] = inp["b_rgate"][0, ch]
        vec[:, 8 * ct + 6] = inp["b_igate"][0, ch]
        vec[:, 8 * ct + 7] = inp["lru_lambda"][0, ch]
    return {
        "xT": np.ascontiguousarray(inp["x"][b].T),
        "w1": w1,
        "vec1": vec,
        "wr": np.ascontiguousarray(inp["w_rgate"][0, 4 * g:4 * g + 4]),
        "wi": np.ascontiguousarray(inp["w_igate"][0, 4 * g:4 * g + 4]),
        "gm": np.ascontiguousarray(inp["norm_mix_g"][0].reshape(8, 128).T),
        "cst": _consts(),
    }


def build_p1_program(dbg=False):
    nc = bass.Bass("TRN2", target_bir_lowering=False)
    dr = {}
    if dbg:
        dr["dbgF"] = nc.dram_tensor("dbgF", [128, Ctx.FSZ], F32, kind="ExternalOutput").ap()
        dr["dbgB"] = nc.dram_tensor("dbgB", [128, Ctx.BSZ], BF16, kind="ExternalOutput").ap()
        dr["dbgR"] = nc.dram_tensor("dbgR", [128, Ctx.RSZ], BF16, kind="ExternalOutput").ap()
    for nm, shp in (("xT", [D, S]), ("w1", [D, 1280]), ("vec1", [128, 16]), ("wr", [4, 64, 64]), ("wi", [4, 64, 64]),
                    ("gm", [128, 8]), ("cst", [128, 2432])):
        dr[nm] = nc.dram_tensor(nm, shp, F32, kind="ExternalInput").ap()
    dr["yT"] = nc.dram_tensor("yT", [512, S], BF16, kind="ExternalOutput").ap()
    C = Ctx(nc)
    build_phase1(nc, C, dr)
    global LAST_LAY
    LAST_LAY = list(C.lay)
    C.P.emit()
    C.es.close()
    return nc


def run_phase1(inp):
    nc = build_p1_program()
    in_maps = [_p1_inputs(inp, c // 4, c % 4) for c in range(NCORE)]
    res = run_bass_kernel_spmd(nc, in_maps, core_ids=list(range(NCORE)))
    return [np.asarray(r["yT"]) for r in res.results]


P2_IN = (("xT2", [D, 2048], F32), ("yX", [4, 512, 2048], BF16), ("pT", [256, 2048], F32), ("wg", [D, 2048], F32),
         ("wbl", [D, D], F32), ("wba", [D, D], F32), ("wo", [D, D], F32), ("wpg", [D, D], F32), ("wup", [D, 4096], F32),
         ("wdn", [4096, D], F32), ("wple", [256, D], F32), ("g2", [128, 32], F32))


def _p2_inputs(inp, b, g):
    tok = slice(2048 * g, 2048 * (g + 1))
    g2 = np.concatenate([inp[k].reshape(-1).reshape(8, 128).T for k in ("norm_mix_g", "norm_mlp_g", "norm_ple_g", "norm_final_g")], axis=1)
    return {
        "xT2": np.ascontiguousarray(inp["x"][b, tok].T),
        "pT": np.ascontiguousarray(inp["p"][0, b, tok].T),
        "wg": np.ascontiguousarray(inp["w_in"][0][:, 5120:7168]),
        "wbl": inp["w_br_lru"][0], "wba": inp["w_br_att"][0], "wo": inp["w_out"][0], "wpg": inp["w_ple_gate"][0],
        "wup": inp["w_mlp_up"][0], "wdn": inp["w_mlp_down"][0], "wple": inp["w_ple"][0],
        "g2": np.ascontiguousarray(g2),
    }


def build_p2_program():
    nc = bass.Bass("TRN2", target_bir_lowering=False)
    dr = {}
    for nm, shp, dt in P2_IN:
        dr[nm] = nc.dram_tensor(nm, shp, dt, kind="ExternalInput").ap()
    dr["oT"] = nc.dram_tensor("oT", [D, 2048], F32, kind="ExternalOutput").ap()
    C = Ctx(nc)
    C.mark()
    build_phase2(nc, C, dr)
    C.P.emit()
    C.es.close()
    return nc


def run_phase2(inp, ys):
    nc = build_p2_program()
    in_maps = []
    for c in range(NCORE):
        b, g = c // 4, c % 4
        m = _p2_inputs(inp, b, g)
        m["yX"] = np.ascontiguousarray(np.stack([ys[4 * b + r][:, 2048 * g:2048 * (g + 1)] for r in range(4)]))
        in_maps.append(m)
    res = run_bass_kernel_spmd(nc, in_maps, core_ids=list(range(NCORE)))
    return [np.asarray(r["oT"]) for r in res.results]


def build_fused_program():
    nc = bass.Bass("TRN2", target_bir_lowering=False)
    dr = {}
    for nm, shp in (("xT", [D, S]), ("w1", [D, 1280]), ("vec1", [128, 16]), ("wr", [4, 64, 64]), ("wi", [4, 64, 64]),
                    ("gm", [128, 8]), ("cst", [128, 2432])):
        dr[nm] = nc.dram_tensor(nm, shp, F32, kind="ExternalInput").ap()
    for nm, shp, dt in P2_IN:
        if nm != "yX":
            dr[nm] = nc.dram_tensor(nm, shp, dt, kind="ExternalInput").ap()
    dr["oT"] = nc.dram_tensor("oT", [D, 2048], F32, kind="ExternalOutput").ap()
    yin = [nc.dram_tensor("yin%d" % j, [512, 1024], BF16) for j in range(8)]
    yall = nc.dram_tensor("yall", [8 * 2048, 1024], BF16)
    dr["yin"] = [t.ap() for t in yin]
    dr["yin_tok"] = [[] for j in range(8)]
    dr["yall"] = yall.ap()
    dr["yall_r"] = [Res("yall%d" % j) for j in range(8)]
    dr["wsc_r"] = {}
    for nm, K_, N_ in WSPEC:
        dr[nm + "_b"] = nc.dram_tensor(nm + "_b", [K_, N_], BF16).ap()
        dr["wsc_r"][nm] = Res("wsc_" + nm)
    C = Ctx(nc)

    def issue_cc(j):
        C.P.op("pool", lambda e: e.collective_compute("AllGather", ALU.bypass, replica_groups=[[0, 1, 2, 3], [4, 5, 6, 7]],
                                                       ins=[yin[j].ap().opt()], outs=[yall.ap()[2048 * j:2048 * (j + 1), :].opt()]),
               extra=dr["yin_tok"][j], writes=[dr["yall_r"][j]], dma="ccag%d" % j, cc=True)
    dr["issue_cc"] = issue_cc
    build_phase1(nc, C, dr)
    build_phase2(nc, C, dr)
    C.P.emit()
    C.es.close()
    return nc


def kernel(**inp):
    inp = {k: np.asarray(v) for k, v in inp.items()}
    nc = build_fused_program()
    in_maps = []
    for c in range(NCORE):
        b, g = c // 4, c % 4
        m = _p1_inputs(inp, b, g)
        m.update(_p2_inputs(inp, b, g))
        in_maps.append(m)
    res = run_bass_kernel_spmd(nc, in_maps, core_ids=list(range(NCORE)))
    out = np.empty((2, S, D), np.float32)
    for c in range(NCORE):
        b, g = c // 4, c % 4
        out[b, 2048 * g:2048 * (g + 1), :] = np.asarray(res.results[c]["oT"]).T
    return out
```

```python
import numpy as np
import ml_dtypes
from contextlib import ExitStack
import concourse.bass as bass
import concourse.mybir as mybir
from concourse.bass_utils import run_bass_kernel_spmd

F32 = mybir.dt.float32
BF16 = mybir.dt.bfloat16
AF = mybir.ActivationFunctionType
ALU = mybir.AluOpType

S = 8192
D = 1024
NCORE = 8
EPS = 1e-6
GELU_K = 1.5957691216057308


class Res:
    __slots__ = ("name", "w", "r")

    def __init__(self, name):
        self.name = name
        self.w = None
        self.r = {}


class Prog:
    def __init__(self, nc, es):
        self.nc = nc
        self.es = es
        self.eng = {"pe": nc.tensor, "act": nc.scalar, "dve": nc.vector, "pool": nc.gpsimd, "sp": nc.sync}
        self.stream = {k: [] for k in self.eng}
        self.semh = {}
        self.cnt = {}
        self.known = {k: {} for k in self.eng}
        self.cap = None
        for k in self.eng:
            self._sem("E_" + k)

    def _sem(self, name):
        if name not in self.semh:
            self.semh[name] = self.es.enter_context(self.nc.semaphore(name))
            self.cnt[name] = 0
        return name

    def op(self, eng, fn, reads=(), writes=(), dma=None, cc=False, nowaw=False, extra=None, tokout=None):
        if self.cap is not None:
            self.cap.append((eng, fn, list(reads), list(writes), dma, cc, nowaw, extra, tokout))
            return None
        own = "E_" + eng
        deps = {}

        def add(tok):
            if tok is None:
                return
            n, v = tok
            if deps.get(n, 0) < v:
                deps[n] = v

        for r in reads:
            add(r.w)
        for t_ in (extra or ()):
            add(t_)
        for w in writes:
            if w.w is not None and w.w[0] != own and not nowaw:
                add(w.w)
            for n, v in w.r.items():
                if n != own:
                    add((n, v))
        if cc:
            sn = self._sem("D_" + dma)
            inc = 1
        elif dma is not None:
            sn = self._sem("D_" + dma)
            inc = 16
        else:
            sn = own
            inc = 1
        self.cnt[sn] += inc
        tok = (sn, self.cnt[sn])
        waits = []
        for n, v in deps.items():
            if eng == "pe" and n == own:
                continue
            if self.known[eng].get(n, 0) >= v:
                continue
            self.known[eng][n] = v
            waits.append((n, v))
        self.stream[eng].append((waits, fn, sn, inc))
        for r in reads:
            if r.r.get(sn, 0) < tok[1]:
                r.r[sn] = tok[1]
        for w in writes:
            w.w = tok
            w.r = {}
        if tokout is not None:
            tokout.append(tok)
        return tok

    def barrier(self):
        allc = [(n, v) for n, v in self.cnt.items() if v > 0]
        for k in self.eng:
            waits = []
            for n, v in allc:
                if self.known[k].get(n, 0) >= v:
                    continue
                self.known[k][n] = v
                waits.append((n, v))
            self.stream[k].append((waits, None, None, 0))

    def emit(self):
        with self.nc.Block() as block:
            def mk(k):
                def body(e):
                    for waits, fn, sn, inc in self.stream[k]:
                        for n, v in waits:
                            e.wait_ge(self.semh[n], v)
                        if fn is not None:
                            ins = fn(e)
                            if inc == 1 and sn.startswith("D_"):
                                ins.then_inc(self.semh[sn])
                            else:
                                ins.then_inc(self.semh[sn], inc)
                return body
            block.tensor(mk("pe"))
            block.scalar(mk("act"))
            block.vector(mk("dve"))
            block.gpsimd(mk("pool"))
            block.sync(mk("sp"))


class LazyBank:
    def __init__(self):
        self.ap = None
        self.res = None

    def __getitem__(self, idx):
        return self.ap[idx]


class Ctx:
    FSZ = 17472
    BSZ = 21504
    RSZ = 49152

    def __init__(self, nc):
        self.nc = nc
        self.es = ExitStack()
        self.P = Prog(nc, self.es)
        self.banks = []
        self.ps = self.es.enter_context(nc.psum_tensor("psall", [128, 4096], F32))
        for i in range(8):
            self.banks.append((self.ps[:, 512 * i:512 * (i + 1)], Res("bank%d" % i)))
        self.nb = 0
        self.nbg = 0
        self.pools = {
            "f": [self.es.enter_context(nc.sbuf_tensor("poolF", [128, self.FSZ], F32)), 0, self.FSZ],
            "b": [self.es.enter_context(nc.sbuf_tensor("poolB", [128, self.BSZ], BF16)), 0, self.BSZ],
            "r": [self.es.enter_context(nc.sbuf_tensor("poolR", [128, self.RSZ], BF16)), 0, self.RSZ],
        }
        self.uid = 0
        self.lay = []

    def bank(self, kind="n"):
        if self.P.cap is not None:
            lb = LazyBank()
            lb.kind = kind
            return lb, lb
        b = self.banks[self.nb % 8]
        self.nb += 1
        return b

    def replay(self, pend):
        for eng, fn, reads, writes, dma, cc, nowaw, extra, tokout in pend:
            for lst in (reads, writes):
                for k, r in enumerate(lst):
                    if isinstance(r, LazyBank):
                        if r.ap is None:
                            if getattr(r, "kind", "n") == "g":
                                r.ap, r.res = self.banks[4 + self.nbg % 4]
                                self.nbg += 1
                            else:
                                r.ap, r.res = self.banks[self.nb % 4]
                                self.nb += 1
                        lst[k] = r.res
            self.P.op(eng, fn, reads, writes, dma, cc, nowaw, extra, tokout)

    def sb(self, name, shape, dt, es=None, pool=None):
        if pool is None:
            pool = "f" if dt == F32 else "b"
        pl = self.pools[pool]
        n = 1
        for d in shape[1:]:
            n *= d
        n_al = (n + 1) // 2 * 2
        assert pl[1] + n_al <= pl[2], (name, pool, pl[1], n_al, pl[2])
        ap = pl[0][:, pl[1]:pl[1] + n]
        pl[1] += n_al
        if len(shape) == 3:
            ap = ap.rearrange("p (c t) -> p c t", c=shape[1])
        self.uid += 1
        self.lay.append((name, pool, pl[1] - n_al, n))
        return ap, Res("%s_%d" % (name, self.uid))

    def mark(self):
        self.keep = {k: v[1] for k, v in self.pools.items()}

    def reset(self, *pools):
        for p in pools:
            self.pools[p][1] = self.keep.get(p, 0)


def _rms_tile(C, x_t, x_r, g_t, g_r, ones_t, ones_r, sq, rstd, lnv, hT, TT, eps_t):
    P = C.P
    sq_t, sq_r = sq
    rs_t, rs_r = rstd
    ln_t, ln_r = lnv
    h_t, h_r = hT
    P.op("act", lambda e: e.activation(out=sq_t[:, :, :], in_=x_t[:, :, :], func=AF.Square),
         reads=[x_r], writes=[sq_r])
    bk, bk_r = C.bank()

    def mm(e):
        ins = None
        for c in range(8):
            ins = e.matmul(bk[:, 0:TT], lhsT=ones_t[:, :], rhs=sq_t[:, c, :], start=(c == 0), stop=(c == 7))
        return ins
    P.op("pe", mm, reads=[sq_r, ones_r], writes=[bk_r])
    eps_t, eps_r = eps_t
    P.op("act", lambda e: e.activation(out=ln_t[:, :], in_=bk[:, 0:TT], func=AF.Ln, scale=1.0 / D, bias=eps_t[:, 0:1]),
         reads=[bk_r, eps_r], writes=[ln_r])
    P.op("act", lambda e: e.activation(out=rs_t[:, :], in_=ln_t[:, :], func=AF.Exp, scale=-0.5),
         reads=[ln_r], writes=[rs_r])

    def hh(e):
        ins = None
        for c in range(8):
            ins = e.scalar_tensor_tensor(out=h_t[:, c, :], in0=x_t[:, c, :], scalar=g_t[:, c:c + 1], in1=rs_t[:, :],
                                         op0=ALU.mult, op1=ALU.mult)
        return ins
    P.op("dve", hh, reads=[x_r, g_r, rs_r], writes=[h_r])


def build_phase1(nc, C, dr):
    P = C.P
    es1 = None
    TT = 256
    NT = S // TT

    qT, qT_r = C.sb("qT", [128, 2, S], BF16, pool="r")
    kT, kT_r = C.sb("kT", [128, 2, S], BF16, pool="r")
    vS, vS_r = C.sb("vS", [128, 64, 256], BF16, pool="r")
    ones, ones_r = C.sb("ones", [128, 128], BF16)
    eps_t, eps_r = C.sb("eps", [128, 1], F32)
    one_t, one_r = C.sb("one1", [128, 1], F32)
    cst, cst_r = C.sb("cstb", [128, 2432], BF16)
    C.mark()

    w1b, w1b_r = C.sb("w1b", [128, 8, 1280], BF16, es1)
    wst = [C.sb("wst%d" % i, [128, 1280], F32, es1) for i in range(2)]
    gm, gm_r = C.sb("gm", [128, 8], F32, es1)
    vec, vec_r = C.sb("vec", [128, 16], F32, es1)
    dv, dv_r = C.sb("dv", [128, 8], F32, es1)
    bdst, bdst_r = C.sb("bdst", [128, 4, 128], F32, es1)
    bd, bd_r = C.sb("bd", [128, 4, 128], BF16, es1)

    P.op("pool", lambda e: e.memset(ones[:, :], 1.0), writes=[ones_r])
    P.op("pool", lambda e: e.memset(eps_t[:, :], EPS), writes=[eps_r])
    P.op("pool", lambda e: e.memset(one_t[:, :], 1.0), writes=[one_r])
    P.op("pool", lambda e: e.memset(bdst[:, :, :], 0.0), writes=[bdst_r])
    P.op("sp", lambda e: e.dma_start(out=gm[:, :], in_=dr["gm"][:, :]), writes=[gm_r], dma="gm")
    P.op("sp", lambda e: e.dma_start(out=vec[:, :], in_=dr["vec1"][:, :]), writes=[vec_r], dma="vec")
    for j in range(4):
        ct, hb = j // 2, j % 2
        for nm, base in (("wr", 0), ("wi", 2)):
            P.op("sp", lambda e, nm=nm, j=j, ct=ct, hb=hb, base=base: e.dma_start(
                out=bdst[64 * hb:64 * hb + 64, base + ct, 64 * hb:64 * hb + 64], in_=dr[nm][j, :, :]),
                writes=[bdst_r], dma="bdst")
    P.op("dve", lambda e: e.tensor_copy(out=bd[:, :, :], in_=bdst[:, :, :]), reads=[bdst_r], writes=[bd_r])
    for i in range(2):
        st_t, st_r = wst[i]
        P.op("sp", lambda e, i=i, st_t=st_t: e.dma_start(out=st_t[:, 0:1216], in_=dr["cst"][:, 1216 * i:1216 * (i + 1)]),
             writes=[st_r], dma="wst%d" % i)
        P.op("dve", lambda e, i=i, st_t=st_t: e.tensor_copy(out=cst[:, 1216 * i:1216 * (i + 1)], in_=st_t[:, 0:1216]),
             reads=[st_r], writes=[cst_r])
    for c in range(8):
        st_t, st_r = wst[c % 2]
        P.op("sp", lambda e, c=c, st_t=st_t: e.dma_start(out=st_t[:, :], in_=dr["w1"][128 * c:128 * (c + 1), :]),
             writes=[st_r], dma="wst%d" % (c % 2))
        P.op("pool" if c % 2 else "dve", lambda e, c=c, st_t=st_t: e.tensor_copy(out=w1b[:, c, :], in_=st_t[:, :]),
             reads=[st_r], writes=[w1b_r])
    tmpv, tmpv_r = C.sb("tmpv", [128, 4], F32, es1)
    for ct in range(2):
        P.op("act", lambda e, ct=ct: e.activation(out=tmpv[:, ct:ct + 1], in_=vec[:, 8 * ct + 7:8 * ct + 8], func=AF.Exp, scale=-1.0),
             reads=[vec_r], writes=[tmpv_r])
        P.op("act", lambda e, ct=ct: e.activation(out=tmpv[:, 2 + ct:3 + ct], in_=tmpv[:, ct:ct + 1], func=AF.Ln, bias=one_t[:, 0:1]),
             reads=[tmpv_r, one_r], writes=[tmpv_r])
        P.op("dve", lambda e, ct=ct: e.tensor_scalar(out=dv[:, 4 * ct:4 * ct + 1], in0=tmpv[:, 2 + ct:3 + ct], scalar1=-8.0, scalar2=None, op0=ALU.mult),
             reads=[tmpv_r], writes=[dv_r])
        P.op("dve", lambda e, ct=ct: e.tensor_scalar(out=dv[:, 4 * ct + 1:4 * ct + 2], in0=tmpv[:, 2 + ct:3 + ct], scalar1=-16.0, scalar2=None, op0=ALU.mult),
             reads=[tmpv_r], writes=[dv_r])
        P.op("dve", lambda e, ct=ct: e.tensor_scalar(out=dv[:, 4 * ct + 2:4 * ct + 4], in0=vec[:, 8 * ct + 5:8 * ct + 7], scalar1=-1.0, scalar2=None, op0=ALU.mult),
             reads=[vec_r], writes=[dv_r])

    xs = [C.sb("xs%d" % i, [128, 8, TT], F32, es1) for i in range(2)]
    sq = C.sb("sq", [128, 8, TT], BF16, es1)
    rstd = C.sb("rstd", [128, TT], F32, es1)
    lnv = C.sb("lnv", [128, TT], F32, es1)
    hTs = [C.sb("hT%d" % i, [128, 8, TT], BF16, es1) for i in range(2)]
    uxb = [[C.sb("ux%d_%d" % (ct, i), [128, TT + 4], F32, es1) for i in range(2)] for ct in range(2)]
    ugb = [[C.sb("ug%d_%d" % (ct, i), [128, TT], F32, es1) for i in range(2)] for ct in range(2)]
    T = {}
    for nm in ("c", "r", "i", "a", "a2", "m", "u", "w", "e", "gg"):
        for ct in range(2):
            T[nm, ct] = C.sb("t_%s%d" % (nm, ct), [128, TT], F32, es1)
    rscr = C.sb("rscr", [128, TT], F32, es1)
    cbf = [C.sb("cbf%d" % ct, [128, TT], BF16, es1) for ct in range(2)]
    hh = [[C.sb("h%d_%d" % (ct, i), [128, TT], F32, es1) for i in range(2)] for ct in range(2)]
    ysb = [[C.sb("y%d_%d" % (ct, i), [128, TT], BF16, es1) for i in range(2)] for ct in range(2)]
    for ct in range(2):
        P.op("pool", lambda e, ct=ct: e.memset(uxb[ct][0][0][:, 0:3], 0.0), writes=[uxb[ct][0][1]])

    xTv = dr["xT"].rearrange("(c p) t -> p c t", p=128)

    def load_x(tt):
        x_t, x_r = xs[tt % 2]
        P.op("sp", lambda e: e.dma_start(out=x_t[:, :, :], in_=xTv[:, :, tt * TT:(tt + 1) * TT]),
             writes=[x_r], dma="xs%d" % (tt % 2))

    def tile(tt):
        t0 = tt * TT
        if tt + 1 < NT:
            load_x(tt + 1)
        x_t, x_r = xs[tt % 2]
        h_t, h_r = hTs[tt % 2]
        ux = [uxb[ct][tt % 2] for ct in range(2)]
        ug = [ugb[ct][tt % 2] for ct in range(2)]
        _rms_tile(C, x_t, x_r, gm, gm_r, ones, ones_r, sq, rstd, lnv, (h_t, h_r), TT, (eps_t, eps_r))
        if P.cap is not None:
            split.append(len(P.cap))
        for n in range(8):
            bk, bk_r = C.bank()

            def mm(e, n=n, bk=bk):
                ins = None
                for c in range(8):
                    ins = e.matmul(bk[:, 0:TT], lhsT=w1b[:, c, 128 * n:128 * (n + 1)], rhs=h_t[:, c, :],
                                   start=(c == 0), stop=(c == 7))
                return ins
            P.op("pe", mm, reads=[w1b_r, h_r], writes=[bk_r])
            if n < 2:
                P.op("dve", lambda e, n=n, bk=bk: e.tensor_copy(out=ux[n][0][:, 3:3 + TT], in_=bk[:, 0:TT]),
                     reads=[bk_r], writes=[ux[n][1]])
                if tt > 0:
                    up_t, up_r = uxb[n][(tt + 1) % 2]
                    P.op("pool", lambda e, n=n, up_t=up_t: e.tensor_copy(out=ux[n][0][:, 0:3], in_=up_t[:, TT:TT + 3]),
                         reads=[up_r], writes=[ux[n][1]])
            elif n < 4:
                P.op("dve", lambda e, n=n, bk=bk: e.tensor_copy(out=ug[n - 2][0][:, :], in_=bk[:, 0:TT]),
                     reads=[bk_r], writes=[ug[n - 2][1]])
            elif n < 6:
                P.op("dve", lambda e, n=n, bk=bk: e.tensor_scalar(out=qT[:, n - 4, t0:t0 + TT], in0=bk[:, 0:TT], scalar1=0.125, scalar2=None, op0=ALU.mult),
                     reads=[bk_r], writes=[qT_r])
            else:
                P.op("dve", lambda e, n=n, bk=bk: e.tensor_copy(out=kT[:, n - 6, t0:t0 + TT], in_=bk[:, 0:TT]),
                     reads=[bk_r], writes=[kT_r])
        for sub in range(TT // 128):
            bk, bk_r = C.bank()
            kb = (t0 // 128) + sub

            def mmv(e, sub=sub, bk=bk):
                ins = None
                for c in range(8):
                    ins = e.matmul(bk[:, 0:256], lhsT=h_t[:, c, 128 * sub:128 * (sub + 1)], rhs=w1b[:, c, 1024:1280],
                                   start=(c == 0), stop=(c == 7))
                return ins
            P.op("pe", mmv, reads=[w1b_r, h_r], writes=[bk_r])
            P.op("dve", lambda e, kb=kb, bk=bk: e.tensor_copy(out=vS[:, kb, :], in_=bk[:, 0:256]),
                 reads=[bk_r], writes=[vS_r])
        if P.cap is not None:
            split.append(len(P.cap))
        def TT_(nm, ct):
            return T[nm, ct]
        for k in (3, 2, 1, 0):
            for ct in range(2):
                c_t, c_r = T["c", ct]
                u_t, u_r = ux[ct]
                if k == 3:
                    P.op("dve", lambda e, ct=ct, c_t=c_t, u_t=u_t: e.tensor_scalar(
                        out=c_t[:, :], in0=u_t[:, 3:3 + TT], scalar1=vec[:, 8 * ct + 3:8 * ct + 4], scalar2=vec[:, 8 * ct + 4:8 * ct + 5],
                        op0=ALU.mult, op1=ALU.add), reads=[u_r, vec_r], writes=[c_r])
                else:
                    P.op("dve", lambda e, ct=ct, k=k, c_t=c_t, u_t=u_t: e.scalar_tensor_tensor(
                        out=c_t[:, :], in0=u_t[:, k:k + TT], scalar=vec[:, 8 * ct + k:8 * ct + k + 1], in1=c_t[:, :],
                        op0=ALU.mult, op1=ALU.add), reads=[u_r, vec_r, c_r], writes=[c_r])
        for ct in range(2):
            u_t, u_r = ux[ct]
            P.op("pool", lambda e, ct=ct: e.tensor_copy(out=cbf[ct][0][:, :], in_=T["c", ct][0][:, :]),
                 reads=[T["c", ct][1]], writes=[cbf[ct][1]])
        gates = []
        gbanks = []
        for ct in range(2):
            for gi, nm in ((0, "r"), (1, "i")):
                bk, bk_r = C.bank(kind="g")
                P.op("pe", lambda e, ct=ct, gi=gi, bk=bk: e.matmul(bk[:, 0:TT], lhsT=bd[:, 2 * gi + ct, :], rhs=cbf[ct][0][:, :], start=True, stop=True),
                     reads=[bd_r, cbf[ct][1]], writes=[bk_r])
                gbanks.append((ct, gi, nm, bk, bk_r))
        if P.cap is not None:
            split.append(len(P.cap))
        for ct, gi, nm, bk, bk_r in gbanks:
            o_t, o_r = T[nm, ct]
            P.op("act", lambda e, ct=ct, gi=gi, bk=bk, o_t=o_t: e.activation(
                out=o_t[:, :], in_=bk[:, 0:TT], func=AF.Exp, scale=-1.0, bias=dv[:, 4 * ct + 2 + gi:4 * ct + 3 + gi]),
                reads=[bk_r, dv_r], writes=[o_r])
            gates.append((o_t, o_r))
        for o_t, o_r in gates:
            P.op("act", lambda e, o_t=o_t: e.activation(out=o_t[:, :], in_=o_t[:, :], func=AF.Ln, bias=1.0),
                 reads=[o_r], writes=[o_r])
        for o_t, o_r in gates:
            P.op("act", lambda e, o_t=o_t: e.activation(out=o_t[:, :], in_=o_t[:, :], func=AF.Exp, scale=-1.0),
                 reads=[o_r], writes=[o_r])
        for ct in range(2):
            r_t, r_r = T["r", ct]
            P.op("act", lambda e, ct=ct, r_t=r_t: e.activation(out=T["a", ct][0][:, :], in_=r_t[:, :], func=AF.Exp, scale=dv[:, 4 * ct:4 * ct + 1]),
                 reads=[r_r, dv_r], writes=[T["a", ct][1]])
            P.op("act", lambda e, ct=ct, r_t=r_t: e.activation(out=T["a2", ct][0][:, :], in_=r_t[:, :], func=AF.Exp, scale=dv[:, 4 * ct + 1:4 * ct + 2]),
                 reads=[r_r, dv_r], writes=[T["a2", ct][1]])
        for ct in range(2):
            P.op("dve", lambda e, ct=ct: e.tensor_scalar(out=T["a2", ct][0][:, :], in0=T["a2", ct][0][:, :], scalar1=-1.0, scalar2=1.0, op0=ALU.mult, op1=ALU.add),
                 reads=[T["a2", ct][1]], writes=[T["a2", ct][1]])
        for ct in range(2):
            P.op("act", lambda e, ct=ct: e.activation(out=T["m", ct][0][:, :], in_=T["a2", ct][0][:, :], func=AF.Ln),
                 reads=[T["a2", ct][1]], writes=[T["m", ct][1]])
        for ct in range(2):
            P.op("act", lambda e, ct=ct: e.activation(out=T["m", ct][0][:, :], in_=T["m", ct][0][:, :], func=AF.Exp, scale=0.5),
                 reads=[T["m", ct][1]], writes=[T["m", ct][1]])
        for ct in range(2):
            g_t, g_r = ug[ct]
            w_t, w_r = T["w", ct]
            P.op("pool", lambda e, g_t=g_t, w_t=w_t: e.tensor_tensor(out=w_t[:, :], in0=g_t[:, :], in1=g_t[:, :], op=ALU.mult),
                 reads=[g_r], writes=[w_r])
        for ct in range(2):
            w_t, w_r = T["w", ct]
            P.op("pool", lambda e, w_t=w_t: e.tensor_scalar(out=w_t[:, :], in0=w_t[:, :], scalar1=0.044715, scalar2=1.0, op0=ALU.mult, op1=ALU.add),
                 reads=[w_r], writes=[w_r])
        for ct in range(2):
            g_t, g_r = ug[ct]
            w_t, w_r = T["w", ct]
            P.op("pool", lambda e, g_t=g_t, w_t=w_t: e.tensor_tensor(out=w_t[:, :], in0=w_t[:, :], in1=g_t[:, :], op=ALU.mult),
                 reads=[w_r, g_r], writes=[w_r])
        for ct in range(2):
            P.op("act", lambda e, ct=ct: e.activation(out=T["e", ct][0][:, :], in_=T["w", ct][0][:, :], func=AF.Exp, scale=-GELU_K),
                 reads=[T["w", ct][1]], writes=[T["e", ct][1]])
        for ct in range(2):
            P.op("pool", lambda e, ct=ct: e.tensor_tensor(out=T["u", ct][0][:, :], in0=T["m", ct][0][:, :], in1=T["i", ct][0][:, :], op=ALU.mult),
                 reads=[T["m", ct][1], T["i", ct][1]], writes=[T["u", ct][1]])
        for ct in range(2):
            P.op("pool", lambda e, ct=ct: e.tensor_tensor(out=T["u", ct][0][:, :], in0=T["u", ct][0][:, :], in1=T["c", ct][0][:, :], op=ALU.mult),
                 reads=[T["u", ct][1], T["c", ct][1]], writes=[T["u", ct][1]])
        if P.cap is not None:
            split.append(len(P.cap))
        for ct in range(2):
            hc_t, hc_r = hh[ct][tt % 2]
            hp_t, hp_r = hh[ct][(tt + 1) % 2]
            if tt == 0:
                P.op("dve", lambda e, ct=ct, hc_t=hc_t: e.tensor_tensor_scan(
                    out=hc_t[:, :], data0=T["a", ct][0][:, :], data1=T["u", ct][0][:, :], initial=0.0, op0=ALU.mult, op1=ALU.add),
                    reads=[T["a", ct][1], T["u", ct][1]], writes=[hc_r])
            else:
                P.op("dve", lambda e, ct=ct, hc_t=hc_t, hp_t=hp_t: e.tensor_tensor_scan(
                    out=hc_t[:, :], data0=T["a", ct][0][:, :], data1=T["u", ct][0][:, :], initial=hp_t[:, TT - 1:TT], op0=ALU.mult, op1=ALU.add),
                    reads=[T["a", ct][1], T["u", ct][1], hp_r], writes=[hc_r])
        for ct in range(2):
            P.op("act", lambda e, ct=ct: e.activation(out=T["e", ct][0][:, :], in_=T["e", ct][0][:, :], func=AF.Ln, bias=1.0),
                 reads=[T["e", ct][1]], writes=[T["e", ct][1]])
        for ct in range(2):
            P.op("act", lambda e, ct=ct: e.activation(out=T["e", ct][0][:, :], in_=T["e", ct][0][:, :], func=AF.Exp, scale=-1.0),
                 reads=[T["e", ct][1]], writes=[T["e", ct][1]])
        for ct in range(2):
            P.op("pool", lambda e, ct=ct: e.tensor_tensor(out=T["gg", ct][0][:, :], in0=T["e", ct][0][:, :], in1=ug[ct][0][:, :], op=ALU.mult),
                 reads=[T["e", ct][1], ug[ct][1]], writes=[T["gg", ct][1]])
        for ct in range(2):
            y_t, y_r = ysb[ct][tt % 2]
            hc_t, hc_r = hh[ct][tt % 2]
            P.op("pool", lambda e, ct=ct, y_t=y_t, hc_t=hc_t: e.tensor_tensor(out=y_t[:, :], in0=T["gg", ct][0][:, :], in1=hc_t[:, :], op=ALU.mult),
                 reads=[T["gg", ct][1], hc_r], writes=[y_r])
            jj, co = t0 // 1024, t0 % 1024
            P.op("sp", lambda e, ct=ct, y_t=y_t, co=co, jj=jj: e.dma_start(out=dr["yin"][jj][128 * ct:128 * (ct + 1), co:co + TT], in_=y_t[:, :]),
                 reads=[y_r], dma="y%d_%d" % (ct, tt % 2), tokout=dr["yin_tok"][jj])

    pcl = []
    if "wsc_r" in dr:
        for nm, K_, N_ in WSPEC:
            for c in range(K_ // 128):
                pcl.append((nm, c))

    def precast(i):
        nm, c = pcl[i]
        P.op("pool", lambda e: e.dma_start(out=dr[nm + "_b"][128 * c:128 * (c + 1), :], in_=dr[nm][128 * c:128 * (c + 1), :],
                                           max_dma_last_dim=4096),
             writes=[dr["wsc_r"][nm]], dma="pc_" + nm, nowaw=True)

    split = []
    load_x(0)
    npc = 0

    def cap_tile(tt):
        P.cap = []
        tile(tt)
        ops, P.cap = P.cap, None
        m4 = split.pop()
        m3 = split.pop()
        m2 = split.pop()
        m1 = split.pop()
        return ops[:m1], ops[m1:m2], ops[m2:m3], ops[m3:m4], ops[m4:]

    def mix(a, b):
        out = []
        ia = ib = 0
        na, nb_ = len(a), len(b)
        while ia < na or ib < nb_:
            if ib >= nb_ or (ia < na and ia * nb_ <= ib * na):
                out.append(a[ia])
                ia += 1
            else:
                out.append(b[ib])
                ib += 1
        return out

    parts = {}
    parts[0] = cap_tile(0)
    C.replay(parts[0][0])
    C.replay(parts[0][1])
    C.replay(parts[0][2])
    parts[1] = cap_tile(1)
    C.replay(parts[1][0])
    for tt in range(NT):
        if tt + 2 < NT:
            parts[tt + 2] = cap_tile(tt + 2)
            C.replay(parts[tt + 2][0])
        nxtB = parts[tt + 1][1] if tt + 1 < NT else []
        C.replay(mix(nxtB, parts[tt][3]))
        if tt + 1 < NT:
            C.replay(parts[tt + 1][2])
        C.replay(parts[tt][4])
        del parts[tt]
    dr["precast_fn"] = precast
    dr["precast_n"] = len(pcl)
    P.barrier()
    if "dbgF" in dr:
        C.layout = {"vec": 0}
        P.op("sp", lambda e: e.dma_start(out=dr["dbgF"][:, :], in_=C.pools["f"][0][:, :]), dma="dbgF")
        P.op("sp", lambda e: e.dma_start(out=dr["dbgB"][:, :], in_=C.pools["b"][0][:, :]), dma="dbgB")
        P.op("sp", lambda e: e.dma_start(out=dr["dbgR"][:, :], in_=C.pools["r"][0][:, :]), dma="dbgR")
        P.barrier()
        return
    C.reset("f", "b")
    build_attention(nc, C, dr, qT, qT_r, kT, kT_r, vS, vS_r, cst, cst_r, one_t, one_r)


def build_attention(nc, C, dr, qT, qT_r, kT, kT_r, vS, vS_r, cst, cst_r, one_t, one_r):
    P = C.P
    ZB = [(C.ps[:, 1024 * k:1024 * (k + 1)], Res("zb%d" % k)) for k in range(3)]
    O_b = [(C.ps[:, 3072 + 512 * h:3072 + 512 * (h + 1)], Res("ob%d" % h)) for h in range(2)]
    E_s = [C.sb("E%d" % i, [128, 1024], F32) for i in range(2)]
    L_s = [C.sb("Lp%d" % i, [128, 1024], BF16) for i in range(3)]
    A_s = [C.sb("As%d" % i, [128, 1024], BF16) for i in range(3)]
    Sacc = C.sb("Sacc", [128, 1024], F32)
    Sbf = [C.sb("Sbf%d" % j, [128, 1024], BF16) for j in range(2)]
    yst = [C.sb("yst%d" % i, [128, 512], BF16) for i in range(4)]
    triN = cst[:, 0:128]
    onesN = cst[:, 128:256]

    ident = cst[:, 256:384]

    def negmask(j):
        return cst[:, 384 + 512 * j:384 + 512 * (j + 1)]

    steps = []
    for qt in range(S // 512):
        for pair in range(2):
            for kb in range(4 * qt + 3, -1, -1):
                steps.append((pair, qt, kb))
    NB = len(steps)
    nyst = [0]
    sprev = {}
    ns = 0
    for i, (pair, qt, kb) in enumerate(steps):
        first = (kb == 4 * qt + 3)
        second = (kb == 4 * qt + 2)
        if second:
            sprev[i] = L_s[(i - 1) % 3]
        elif not first:
            sprev[i] = Sbf[(ns - 1) % 2]
        if (not first) and kb > 0:
            ns += 1
    nsw = [0]

    def c0of(qt, kb):
        jd = kb - 4 * qt
        return 128 * jd if jd > 0 else 0

    def v3(ap, c0):
        return ap.rearrange("p (h c) -> p h c", h=2)[:, :, c0:512]

    def pe1(i):
        pair, qt, kb = steps[i]
        z_t, z_r = ZB[i % 3]
        jd = kb - 4 * qt
        c0 = c0of(qt, kb)

        def f(e):
            ins = None
            for hs in range(2):
                lo = 64 * hs
                ins = e.matmul(z_t[:, 512 * hs + c0:512 * (hs + 1)], lhsT=kT[lo:lo + 64, pair, 128 * kb:128 * (kb + 1)],
                               rhs=qT[lo:lo + 64, pair, 512 * qt + c0:512 * (qt + 1)], start=True, stop=False, skip_group_check=True)
                if jd >= 0:
                    ins = e.matmul(z_t[:, 512 * hs + c0:512 * (hs + 1)], lhsT=ident, rhs=negmask(jd)[:, c0:512], start=False, stop=False, skip_group_check=True)
            return ins
        P.op("pe", f, reads=[kT_r, qT_r, cst_r], writes=[z_r])

    def act12(i):
        pair, qt, kb = steps[i]
        z_t, z_r = ZB[i % 3]
        e_t, e_r = E_s[i % 2]
        l_t, l_r = L_s[i % 3]
        c0 = c0of(qt, kb)
        if c0 == 0:
            zz, ee, ll = z_t[:, :], e_t[:, :], l_t[:, :]
        else:
            zz, ee, ll = v3(z_t, c0), v3(e_t, c0), v3(l_t, c0)
        P.op("act", lambda e: e.activation(out=ee, in_=zz, func=AF.Exp), reads=[z_r], writes=[e_r])
        P.op("act", lambda e: e.activation(out=ll, in_=ee, func=AF.Ln, bias=1.0), reads=[e_r], writes=[l_r])
        first = (kb == 4 * qt + 3)
        if kb > 0:
            s_t, s_r = Sacc
            ss = s_t[:, :] if c0 == 0 else v3(s_t, c0)
            if first:
                P.op("pool", lambda e: e.memset(s_t[:, :], 0.0), writes=[s_r])
                P.op("dve", lambda e: e.tensor_copy(out=ss, in_=ll), reads=[l_r], writes=[s_r])
            else:
                P.op("dve", lambda e: e.tensor_tensor(out=ss, in0=ss, in1=ll, op=ALU.add),
                     reads=[s_r, l_r], writes=[s_r])
                sb_t, sb_r = Sbf[nsw[0] % 2]
                nsw[0] += 1
                P.op("dve", lambda e: e.tensor_copy(out=sb_t[:, :], in_=s_t[:, :]), reads=[s_r], writes=[sb_r])

    def pe2(i):
        pair, qt, kb = steps[i]
        z_t, z_r = ZB[i % 3]
        l_t, l_r = L_s[i % 3]
        first = (kb == 4 * qt + 3)
        second = (kb == 4 * qt + 2)
        c0 = c0of(qt, kb)
        cc = 384 if second else c0
        rd = [l_r, cst_r, z_r]
        if not first:
            sb_t, sb_r = sprev[i]
            rd.append(sb_r)

        def f(e):
            ins = None
            for hs in range(2):
                cs = slice(512 * hs + c0, 512 * (hs + 1))
                ins = e.matmul(z_t[:, cs], lhsT=triN, rhs=l_t[:, cs], start=False, stop=first, skip_group_check=True)
                if not first:
                    c2 = slice(512 * hs + cc, 512 * (hs + 1))
                    ins = e.matmul(z_t[:, c2], lhsT=onesN, rhs=sb_t[:, c2], start=False, stop=True, skip_group_check=True)
            return ins
        P.op("pe", f, reads=rd, writes=[z_r])

    def act3(i):
        pair, qt, kb = steps[i]
        z_t, z_r = ZB[i % 3]
        as_t, as_r = A_s[i % 3]
        c0 = c0of(qt, kb)
        if c0 == 0:
            zz, aa = z_t[:, :], as_t[:, :]
        else:
            zz, aa = v3(z_t, c0), v3(as_t, c0)
        P.op("act", lambda e: e.activation(out=aa, in_=zz, func=AF.Exp), reads=[z_r], writes=[as_r])

    def pe3(i):
        pair, qt, kb = steps[i]
        as_t, as_r = A_s[i % 3]
        first = (kb == 4 * qt + 3)
        last = (kb == 0)
        c0 = c0of(qt, kb)

        def f(e):
            ins = None
            for hs in range(2):
                head = 2 * pair + hs
                ins = e.matmul(O_b[hs][0][0:64, c0:512], lhsT=vS[:, kb, 64 * head:64 * (head + 1)], rhs=as_t[:, 512 * hs + c0:512 * (hs + 1)],
                               start=first, stop=last, skip_group_check=True)
            return ins
        P.op("pe", f, reads=[vS_r, as_r], writes=[O_b[0][1], O_b[1][1]])
        if last:
            jj, co = qt // 2, 512 * (qt % 2)
            for hs in range(2):
                head = 2 * pair + hs
                y_t, y_r = yst[nyst[0] % 4]
                nyst[0] += 1
                P.op("dve", lambda e, hs=hs, y_t=y_t: e.tensor_copy(out=y_t[0:64, :], in_=O_b[hs][0][0:64, :]), reads=[O_b[hs][1]], writes=[y_r])
                P.op("sp", lambda e, head=head, y_t=y_t: e.dma_start(
                    out=dr["yin"][jj][256 + 64 * head:256 + 64 * (head + 1), co:co + 512], in_=y_t[0:64, :]),
                    reads=[y_r], dma=y_r.name, tokout=dr["yin_tok"][jj])
            if pair == 1 and qt % 2 == 1:
                dr["issue_cc"](jj)

    npcs = dr.get("precast_n", 0)
    pcdone = 0
    for j in range(NB + 3):
        tgt = min(npcs, (npcs * (j + 1) * 5) // (NB * 4) + 1) if npcs else 0
        while pcdone < tgt:
            dr["precast_fn"](pcdone)
            pcdone += 1
        if j < NB:
            pe1(j)
        if 0 <= j - 2 < NB:
            pe2(j - 2)
        if 0 <= j - 3 < NB:
            pe3(j - 3)
        if 0 <= j - 1 < NB:
            act12(j - 1)
        if 0 <= j - 2 < NB:
            act3(j - 2)
    P.barrier()


WSPEC = (("wg", 1024, 2048), ("wbl", 1024, 1024), ("wba", 1024, 1024), ("wo", 1024, 1024), ("wup", 1024, 4096),
         ("wdn", 4096, 1024), ("wpg", 1024, 1024), ("wple", 256, 1024))


def build_phase2(nc, C, dr):
    P = C.P
    TT = 1024
    NS = TT // 512
    NT2 = 2048 // TT
    C.pools["f"][1] = 0
    C.pools["b"][1] = 0
    C.pools["r"][1] = 0
    ones, ones_r = C.sb("ones2", [128, 128], BF16)
    eps_t, eps_r = C.sb("eps2", [128, 1], F32)
    g2, g2_r = C.sb("g2", [128, 32], F32)
    P.op("pool", lambda e: e.memset(ones[:, :], 1.0), writes=[ones_r])
    P.op("pool", lambda e: e.memset(eps_t[:, :], EPS), writes=[eps_r])
    P.op("sp", lambda e: e.dma_start(out=g2[:, :], in_=dr["g2"][:, :]), writes=[g2_r], dma="g2")
    x_t, x_r = C.sb("x2", [128, 8, TT], F32)
    rstd = C.sb("rstd2", [128, 512], F32)
    lnv = C.sb("lnv2", [128, 512], F32)
    tA = C.sb("tA", [128, 4 * NS, 512], F32)
    tB = C.sb("tB", [128, 4 * NS, 512], F32)
    pst = (tA[0][:, 0:4, :].rearrange("p (c a) n -> p c (a n)", c=2), tA[1])
    hT = C.sb("h2T", [128, 8, TT], BF16)
    mg = C.sb("mg", [128, 8, TT], BF16)
    sq = mg
    pb = C.sb("pb", [128, 2, TT], BF16)
    act = C.sb("actT", [128, 32, TT], BF16, pool="r")
    yt = (act[0][:, 0:16, :], act[1])
    wb = [C.sb("wb%d" % i, [128, 8, 512], BF16, pool="r") for i in range(3)]
    wpl = C.sb("wpl", [128, 2, 1024], BF16, pool="r")
    ng = [0]

    def gran(wname, r0, c0, kc=8, gc=512, dst=None):
        i = ng[0]
        ng[0] += 1
        w_t, w_r = dst if dst is not None else wb[i % 3]
        src = dr[wname + "_b"][r0:r0 + 128 * kc, c0:c0 + gc].rearrange("(c p) n -> p c n", p=128)
        wv = w_t if dst is not None else w_t[:, :, 0:gc]
        P.op("sp", lambda e: e.dma_start(out=wv, in_=src), reads=[dr["wsc_r"][wname]], writes=[w_r], dma=w_r.name)
        return w_t, w_r

    def mmgroup(bk, bk_r, w_t, w_r, col, rhs_list, rhs_res, start=True, stop=True):
        n = len(rhs_list)

        def f(e):
            ins = None
            for c in range(n):
                ins = e.matmul(bk[:, :], lhsT=w_t[:, c, 128 * col:128 * (col + 1)], rhs=rhs_list[c],
                               start=(start and c == 0), stop=(stop and c == n - 1))
            return ins
        P.op("pe", f, reads=[w_r] + list(rhs_res), writes=[bk_r])

    xv = dr["xT2"].rearrange("(c p) t -> p c t", p=128)
    ov = dr["oT"].rearrange("(c p) t -> p c t", p=128)
    pv = dr["pT"].rearrange("(c p) t -> p c t", p=128)

    def sl(sub):
        return slice(512 * sub, 512 * (sub + 1))

    def tile2(tt):
        t0 = tt * TT
        y_t, y_r = yt
        P.op("sp", lambda e: e.dma_start(out=x_t[:, :, :], in_=xv[:, :, t0:t0 + TT]), writes=[x_r], dma=x_r.name)
        for r in range(4):
            def ld(e, r=r):
                if "rank" not in dr:
                    dr["rank"] = e.partition_id() % 4
                rank = dr["rank"]
                src = dr["yall"][bass.ds((rank * 2 + tt) * 2048 + 512 * r, 512), :].rearrange("(c p) t -> p c t", p=128)
                return e.dma_start(out=y_t[:, 4 * r:4 * r + 4, :], in_=src)
            P.op("pool", ld, reads=list(dr["yall_r"]), writes=[y_r], dma=y_r.name)
        P.op("sp", lambda e: e.dma_start(out=pst[0][:, :, :], in_=pv[:, :, t0:t0 + TT]), writes=[pst[1]], dma=pst[1].name)
        P.op("pool", lambda e: e.tensor_copy(out=pb[0][:, :, :], in_=pst[0][:, :, :]), reads=[pst[1]], writes=[pb[1]])

        def ylru(sub):
            return [y_t[:, 4 * (c // 2) + (c % 2), sl(sub)] for c in range(8)]

        def yatt(sub):
            return [y_t[:, 4 * (c // 2) + 2 + (c % 2), sl(sub)] for c in range(8)]

        def hl(sub):
            return [hT[0][:, c, sl(sub)] for c in range(8)]

        def rms(gcol, out):
            for sub in range(NS):
                _rms_tile(C, x_t[:, :, sl(sub)], x_r, g2[:, gcol:gcol + 8], g2_r, ones, ones_r,
                          (sq[0][:, :, sl(sub)], sq[1]), rstd, lnv,
                          (out[0][:, :, sl(sub)], out[1]), 512, (eps_t, eps_r))

        rms(0, hT)
        for cg in range(2):
            w_t, w_r = gran("wg", 0, 512 * cg)
            for t in range(4):
                for sub in range(NS):
                    bk, bk_r = C.bank()
                    mmgroup(bk, bk_r, w_t, w_r, t, hl(sub), [hT[1]])
                    P.op("act", lambda e, t=t, sub=sub, bk=bk: e.activation(out=tA[0][:, NS * t + sub, :], in_=bk[:, :], func=AF.Sigmoid),
                         reads=[bk_r], writes=[tA[1]])
            w_t, w_r = gran("wbl", 0, 512 * cg)
            for t in range(4):
                for sub in range(NS):
                    bk, bk_r = C.bank()
                    mmgroup(bk, bk_r, w_t, w_r, t, ylru(sub), [y_r])
                    P.op("dve", lambda e, t=t, sub=sub, bk=bk: e.tensor_tensor(out=tA[0][:, NS * t + sub, :], in0=tA[0][:, NS * t + sub, :], in1=bk[:, :], op=ALU.mult),
                         reads=[bk_r, tA[1]], writes=[tA[1]])
            w_t, w_r = gran("wg", 0, 1024 + 512 * cg)
            for t in range(4):
                for sub in range(NS):
                    bk, bk_r = C.bank()
                    mmgroup(bk, bk_r, w_t, w_r, t, hl(sub), [hT[1]])
                    P.op("act", lambda e, t=t, sub=sub, bk=bk: e.activation(out=tB[0][:, NS * t + sub, :], in_=bk[:, :], func=AF.Sigmoid),
                         reads=[bk_r], writes=[tB[1]])
            w_t, w_r = gran("wba", 0, 512 * cg)
            for t in range(4):
                for sub in range(NS):
                    bk, bk_r = C.bank()
                    mmgroup(bk, bk_r, w_t, w_r, t, yatt(sub), [y_r])
                    P.op("dve", lambda e, t=t, sub=sub, bk=bk: e.tensor_tensor(out=tB[0][:, NS * t + sub, :], in0=tB[0][:, NS * t + sub, :], in1=bk[:, :], op=ALU.mult),
                         reads=[bk_r, tB[1]], writes=[tB[1]])
            for t in range(4):
                P.op("pool", lambda e, cg=cg, t=t: e.tensor_tensor(
                    out=mg[0][:, 4 * cg + t, :], in0=tA[0][:, NS * t:NS * (t + 1), :].rearrange("p s n -> p (s n)"),
                    in1=tB[0][:, NS * t:NS * (t + 1), :].rearrange("p s n -> p (s n)"), op=ALU.add),
                    reads=[tA[1], tB[1]], writes=[mg[1]])
        for cg in range(2):
            w_t, w_r = gran("wo", 0, 512 * cg)
            for t in range(4):
                for sub in range(NS):
                    bk, bk_r = C.bank()
                    mmgroup(bk, bk_r, w_t, w_r, t, [mg[0][:, c, sl(sub)] for c in range(8)], [mg[1]])
                    P.op("dve", lambda e, t=t, cg=cg, sub=sub, bk=bk: e.tensor_tensor(
                        out=x_t[:, 4 * cg + t, sl(sub)], in0=x_t[:, 4 * cg + t, sl(sub)], in1=bk[:, :], op=ALU.add),
                        reads=[bk_r, x_r], writes=[x_r])
        rms(8, hT)
        for cg in range(8):
            w_t, w_r = gran("wup", 0, 512 * cg)
            for t in range(4):
                for sub in range(NS):
                    bk, bk_r = C.bank()
                    mmgroup(bk, bk_r, w_t, w_r, t, hl(sub), [hT[1]])
                    tmp = tA if (t + sub) % 2 == 0 else tB
                    P.op("act", lambda e, bk=bk, tmp=tmp, t=t, sub=sub: e.activation(out=tmp[0][:, NS * t + sub, :], in_=bk[:, :], func=AF.Relu),
                         reads=[bk_r], writes=[tmp[1]])
                    P.op("pool" if (t + sub) % 2 else "dve", lambda e, t=t, cg=cg, tmp=tmp, sub=sub: e.tensor_tensor(
                        out=act[0][:, 4 * cg + t, sl(sub)], in0=tmp[0][:, NS * t + sub, :], in1=tmp[0][:, NS * t + sub, :], op=ALU.mult),
                        reads=[tmp[1]], writes=[act[1]])
        for cg in range(4):
            bks = [[C.bank() for sub in range(NS)] for t in range(2)]
            for kg in range(4):
                w_t, w_r = gran("wdn", 1024 * kg, 256 * cg, gc=256)
                for t in range(2):
                    for sub in range(NS):
                        al = [act[0][:, 8 * kg + c, sl(sub)] for c in range(8)]
                        mmgroup(bks[t][sub][0], bks[t][sub][1], w_t, w_r, t, al, [act[1]], start=(kg == 0), stop=(kg == 3))
            for t in range(2):
                for sub in range(NS):
                    P.op("dve", lambda e, t=t, cg=cg, sub=sub, bk=bks[t][sub][0]: e.tensor_tensor(
                        out=x_t[:, 2 * cg + t, sl(sub)], in0=x_t[:, 2 * cg + t, sl(sub)], in1=bk[:, :], op=ALU.add),
                        reads=[bks[t][sub][1], x_r], writes=[x_r])
        rms(16, hT)
        wp_t, wp_r = gran("wple", 0, 0, kc=2, gc=1024, dst=wpl)
        for cg in range(2):
            w_t, w_r = gran("wpg", 0, 512 * cg)
            for t in range(4):
                for sub in range(NS):
                    bk, bk_r = C.bank()
                    mmgroup(bk, bk_r, w_t, w_r, t, hl(sub), [hT[1]])
                    P.op("act", lambda e, t=t, sub=sub, bk=bk: e.activation(out=tA[0][:, NS * t + sub, :], in_=bk[:, :], func=AF.Sigmoid),
                         reads=[bk_r], writes=[tA[1]])
            for t in range(4):
                for sub in range(NS):
                    bk, bk_r = C.bank()
                    mmgroup(bk, bk_r, wp_t, wp_r, 4 * cg + t, [pb[0][:, c, sl(sub)] for c in range(2)], [pb[1]])
                    P.op("dve", lambda e, t=t, sub=sub, bk=bk: e.tensor_tensor(out=tA[0][:, NS * t + sub, :], in0=tA[0][:, NS * t + sub, :], in1=bk[:, :], op=ALU.mult),
                         reads=[bk_r, tA[1]], writes=[tA[1]])
            for t in range(4):
                P.op("dve", lambda e, cg=cg, t=t: e.tensor_tensor(
                    out=x_t[:, 4 * cg + t, :], in0=x_t[:, 4 * cg + t, :], in1=tA[0][:, NS * t:NS * (t + 1), :].rearrange("p s n -> p (s n)"), op=ALU.add),
                    reads=[tA[1], x_r], writes=[x_r])
        rms(24, (x_t, x_r))
        P.op("sp", lambda e: e.dma_start(out=ov[:, :, t0:t0 + TT], in_=x_t[:, :, :]), reads=[x_r], dma="outst")

    for tt in range(NT2):
        tile2(tt)
    P.barrier()


def _consts():
    c = np.zeros((128, 2432), np.float32)
    jj = np.arange(128)[:, None]
    ss = np.arange(128)[None, :]
    c[:, 0:128] = -(jj >= ss).astype(np.float32)
    c[:, 128:256] = -1.0
    c[:, 256:384] = np.eye(128, dtype=np.float32)
    cc = np.arange(512)[None, :]
    for j in range(4):
        c[:, 384 + 512 * j:384 + 512 * (j + 1)] = -30000.0 * ((128 * j + jj) >= cc).astype(np.float32)
    return c


def _p1_inputs(inp, b, g):
    w_in = inp["w_in"][0]
    sl = slice(256 * g, 256 * (g + 1))
    cols = [w_in[:, 0 * 1024:1 * 1024][:, sl], w_in[:, 1 * 1024:2 * 1024][:, sl], w_in[:, 2 * 1024:3 * 1024][:, sl],
            w_in[:, 3 * 1024:4 * 1024][:, sl], w_in[:, 4 * 1024:5 * 1024][:, sl]]
    w1 = np.ascontiguousarray(np.concatenate(cols, axis=1))
    vec = np.zeros((128, 16), np.float32)
    for ct in range(2):
        ch = slice(256 * g + 128 * ct, 256 * g + 128 * (ct + 1))
        for k in range(4):
            vec[:, 8 * ct + k] = inp["conv_w"][0, k, ch]
        vec[:, 8 * ct + 4] = inp["conv_b"][0, ch]
        vec[:, 8 * ct + 5] = inp["b_rgate"][0, ch]
        vec[:, 8 * ct + 6] = inp["b_igate"][0, ch]
        vec[:, 8 * ct + 7] = inp["lru_lambda"][0, ch]
    return {
        "xT": np.ascontiguousarray(inp["x"][b].T),
        "w1": w1,
        "vec1": vec,
        "wr": np.ascontiguousarray(inp["w_rgate"][0, 4 * g:4 * g + 4]),
        "wi": np.ascontiguousarray(inp["w_igate"][0, 4 * g:4 * g + 4]),
        "gm": np.ascontiguousarray(inp["norm_mix_g"][0].reshape(8, 128).T),
        "cst": _consts(),
    }


def build_p1_program(dbg=False):
    nc = bass.Bass("TRN2", target_bir_lowering=False)
    dr = {}
    if dbg:
        dr["dbgF"] = nc.dram_tensor("dbgF", [128, Ctx.FSZ], F32, kind="ExternalOutput").ap()
        dr["dbgB"] = nc.dram_tensor("dbgB", [128, Ctx.BSZ], BF16, kind="ExternalOutput").ap()
        dr["dbgR"] = nc.dram_tensor("dbgR", [128, Ctx.RSZ], BF16, kind="ExternalOutput").ap()
    for nm, shp in (("xT", [D, S]), ("w1", [D, 1280]), ("vec1", [128, 16]), ("wr", [4, 64, 64]), ("wi", [4, 64, 64]),
                    ("gm", [128, 8]), ("cst", [128, 2432])):
        dr[nm] = nc.dram_tensor(nm, shp, F32, kind="ExternalInput").ap()
    dr["yT"] = nc.dram_tensor("yT", [512, S], BF16, kind="ExternalOutput").ap()
    C = Ctx(nc)
    build_phase1(nc, C, dr)
    global LAST_LAY
    LAST_LAY = list(C.lay)
    C.P.emit()
    C.es.close()
    return nc


def run_phase1(inp):
    nc = build_p1_program()
    in_maps = [_p1_inputs(inp, c // 4, c % 4) for c in range(NCORE)]
    res = run_bass_kernel_spmd(nc, in_maps, core_ids=list(range(NCORE)))
    return [np.asarray(r["yT"]) for r in res.results]


P2_IN = (("xT2", [D, 2048], F32), ("yX", [4, 512, 2048], BF16), ("pT", [256, 2048], F32), ("wg", [D, 2048], F32),
         ("wbl", [D, D], F32), ("wba", [D, D], F32), ("wo", [D, D], F32), ("wpg", [D, D], F32), ("wup", [D, 4096], F32),
         ("wdn", [4096, D], F32), ("wple", [256, D], F32), ("g2", [128, 32], F32))


def _p2_inputs(inp, b, g):
    tok = slice(2048 * g, 2048 * (g + 1))
    g2 = np.concatenate([inp[k].reshape(-1).reshape(8, 128).T for k in ("norm_mix_g", "norm_mlp_g", "norm_ple_g", "norm_final_g")], axis=1)
    return {
        "xT2": np.ascontiguousarray(inp["x"][b, tok].T),
        "pT": np.ascontiguousarray(inp["p"][0, b, tok].T),
        "wg": np.ascontiguousarray(inp["w_in"][0][:, 5120:7168]),
        "wbl": inp["w_br_lru"][0], "wba": inp["w_br_att"][0], "wo": inp["w_out"][0], "wpg": inp["w_ple_gate"][0],
        "wup": inp["w_mlp_up"][0], "wdn": inp["w_mlp_down"][0], "wple": inp["w_ple"][0],
        "g2": np.ascontiguousarray(g2),
    }


def build_p2_program():
    nc = bass.Bass("TRN2", target_bir_lowering=False)
    dr = {}
    for nm, shp, dt in P2_IN:
        dr[nm] = nc.dram_tensor(nm, shp, dt, kind="ExternalInput").ap()
    dr["oT"] = nc.dram_tensor("oT", [D, 2048], F32, kind="ExternalOutput").ap()
    C = Ctx(nc)
    C.mark()
    build_phase2(nc, C, dr)
    C.P.emit()
    C.es.close()
    return nc


def run_phase2(inp, ys):
    nc = build_p2_program()
    in_maps = []
    for c in range(NCORE):
        b, g = c // 4, c % 4
        m = _p2_inputs(inp, b, g)
        m["yX"] = np.ascontiguousarray(np.stack([ys[4 * b + r][:, 2048 * g:2048 * (g + 1)] for r in range(4)]))
        in_maps.append(m)
    res = run_bass_kernel_spmd(nc, in_maps, core_ids=list(range(NCORE)))
    return [np.asarray(r["oT"]) for r in res.results]


def build_fused_program():
    nc = bass.Bass("TRN2", target_bir_lowering=False)
    dr = {}
    for nm, shp in (("xT", [D, S]), ("w1", [D, 1280]), ("vec1", [128, 16]), ("wr", [4, 64, 64]), ("wi", [4, 64, 64]),
                    ("gm", [128, 8]), ("cst", [128, 2432])):
        dr[nm] = nc.dram_tensor(nm, shp, F32, kind="ExternalInput").ap()
    for nm, shp, dt in P2_IN:
        if nm != "yX":
            dr[nm] = nc.dram_tensor(nm, shp, dt, kind="ExternalInput").ap()
    dr["oT"] = nc.dram_tensor("oT", [D, 2048], F32, kind="ExternalOutput").ap()
    yin = [nc.dram_tensor("yin%d" % j, [512, 1024], BF16) for j in range(8)]
    yall = nc.dram_tensor("yall", [8 * 2048, 1024], BF16)
    dr["yin"] = [t.ap() for t in yin]
    dr["yin_tok"] = [[] for j in range(8)]
    dr["yall"] = yall.ap()
    dr["yall_r"] = [Res("yall%d" % j) for j in range(8)]
    dr["wsc_r"] = {}
    for nm, K_, N_ in WSPEC:
        dr[nm + "_b"] = nc.dram_tensor(nm + "_b", [K_, N_], BF16).ap()
        dr["wsc_r"][nm] = Res("wsc_" + nm)
    C = Ctx(nc)

    def issue_cc(j):
        C.P.op("pool", lambda e: e.collective_compute("AllGather", ALU.bypass, replica_groups=[[0, 1, 2, 3], [4, 5, 6, 7]],
                                                       ins=[yin[j].ap().opt()], outs=[yall.ap()[2048 * j:2048 * (j + 1), :].opt()]),
               extra=dr["yin_tok"][j], writes=[dr["yall_r"][j]], dma="ccag%d" % j, cc=True)
    dr["issue_cc"] = issue_cc
    build_phase1(nc, C, dr)
    build_phase2(nc, C, dr)
    C.P.emit()
    C.es.close()
    return nc


def kernel(**inp):
    inp = {k: np.asarray(v) for k, v in inp.items()}
    nc = build_fused_program()
    in_maps = []
    for c in range(NCORE):
        b, g = c // 4, c % 4
        m = _p1_inputs(inp, b, g)
        m.update(_p2_inputs(inp, b, g))
        in_maps.append(m)
    res = run_bass_kernel_spmd(nc, in_maps, core_ids=list(range(NCORE)))
    out = np.empty((2, S, D), np.float32)
    for c in range(NCORE):
        b, g = c // 4, c % 4
        out[b, 2048 * g:2048 * (g + 1), :] = np.asarray(res.results[c]["oT"]).T
    return out
```

```python
import numpy as np
import ml_dtypes
from contextlib import ExitStack
import concourse.bass as bass
import concourse.mybir as mybir
from concourse.bass_utils import run_bass_kernel_spmd

F32 = mybir.dt.float32
BF16 = mybir.dt.bfloat16
AF = mybir.ActivationFunctionType
ALU = mybir.AluOpType

S = 8192
D = 1024
NCORE = 8
EPS = 1e-6
GELU_K = 1.5957691216057308


class Res:
    __slots__ = ("name", "w", "r")

    def __init__(self, name):
        self.name = name
        self.w = None
        self.r = {}


class Prog:
    def __init__(self, nc, es):
        self.nc = nc
        self.es = es
        self.eng = {"pe": nc.tensor, "act": nc.scalar, "dve": nc.vector, "pool": nc.gpsimd, "sp": nc.sync}
        self.stream = {k: [] for k in self.eng}
        self.semh = {}
        self.cnt = {}
        self.known = {k: {} for k in self.eng}
        self.cap = None
        for k in self.eng:
            self._sem("E_" + k)

    def _sem(self, name):
        if name not in self.semh:
            self.semh[name] = self.es.enter_context(self.nc.semaphore(name))
            self.cnt[name] = 0
        return name

    def op(self, eng, fn, reads=(), writes=(), dma=None, cc=False, nowaw=False, extra=None, tokout=None):
        if self.cap is not None:
            self.cap.append((eng, fn, list(reads), list(writes), dma, cc, nowaw, extra, tokout))
            return None
        own = "E_" + eng
        deps = {}

        def add(tok):
            if tok is None:
                return
            n, v = tok
            if deps.get(n, 0) < v:
                deps[n] = v

        for r in reads:
            add(r.w)
        for t_ in (extra or ()):
            add(t_)
        for w in writes:
            if w.w is not None and not nowaw:
                add(w.w)
            for n, v in w.r.items():
                add((n, v))
        if cc:
            sn = self._sem("D_" + dma)
            inc = 1
        elif dma is not None:
            sn = self._sem("D_" + dma)
            inc = 16
        else:
            sn = own
            inc = 1
        self.cnt[sn] += inc
        tok = (sn, self.cnt[sn])
        waits = []
        for n, v in deps.items():
            if eng == "pe" and n == own:
                continue
            if self.known[eng].get(n, 0) >= v:
                continue
            self.known[eng][n] = v
            waits.append((n, v))
        self.stream[eng].append((waits, fn, sn, inc))
        for r in reads:
            if r.r.get(sn, 0) < tok[1]:
                r.r[sn] = tok[1]
        for w in writes:
            w.w = tok
            w.r = {}
        if tokout is not None:
            tokout.append(tok)
        return tok

    def barrier(self):
        allc = [(n, v) for n, v in self.cnt.items() if v > 0]
        for k in self.eng:
            waits = []
            for n, v in allc:
                if self.known[k].get(n, 0) >= v:
                    continue
                self.known[k][n] = v
                waits.append((n, v))
            self.stream[k].append((waits, None, None, 0))

    def emit(self):
        with self.nc.Block() as block:
            def mk(k):
                def body(e):
                    for waits, fn, sn, inc in self.stream[k]:
                        for n, v in waits:
                            e.wait_ge(self.semh[n], v)
                        if fn is not None:
                            ins = fn(e)
                            if inc == 1 and sn.startswith("D_"):
                                ins.then_inc(self.semh[sn])
                            else:
                                ins.then_inc(self.semh[sn], inc)
                return body
            block.tensor(mk("pe"))
            block.scalar(mk("act"))
            block.vector(mk("dve"))
            block.gpsimd(mk("pool"))
            block.sync(mk("sp"))


class LazyBank:
    def __init__(self):
        self.ap = None
        self.res = None

    def __getitem__(self, idx):
        return self.ap[idx]


class Ctx:
    FSZ = 17472
    BSZ = 21504
    RSZ = 49152

    def __init__(self, nc):
        self.nc = nc
        self.es = ExitStack()
        self.P = Prog(nc, self.es)
        self.banks = []
        self.ps = self.es.enter_context(nc.psum_tensor("psall", [128, 4096], F32))
        for i in range(8):
            self.banks.append((self.ps[:, 512 * i:512 * (i + 1)], Res("bank%d" % i)))
        self.nb = 0
        self.nbg = 0
        self.pools = {
            "f": [self.es.enter_context(nc.sbuf_tensor("poolF", [128, self.FSZ], F32)), 0, self.FSZ],
            "b": [self.es.enter_context(nc.sbuf_tensor("poolB", [128, self.BSZ], BF16)), 0, self.BSZ],
            "r": [self.es.enter_context(nc.sbuf_tensor("poolR", [128, self.RSZ], BF16)), 0, self.RSZ],
        }
        self.uid = 0
        self.lay = []

    def bank(self, kind="n"):
        if self.P.cap is not None:
            lb = LazyBank()
            lb.kind = kind
            return lb, lb
        b = self.banks[self.nb % 8]
        self.nb += 1
        return b

    def replay(self, pend):
        for eng, fn, reads, writes, dma, cc, nowaw, extra, tokout in pend:
            for lst in (reads, writes):
                for k, r in enumerate(lst):
                    if isinstance(r, LazyBank):
                        if r.ap is None:
                            if getattr(r, "kind", "n") == "g":
                                r.ap, r.res = self.banks[4 + self.nbg % 4]
                                self.nbg += 1
                            else:
                                r.ap, r.res = self.banks[self.nb % 4]
                                self.nb += 1
                        lst[k] = r.res
            self.P.op(eng, fn, reads, writes, dma, cc, nowaw, extra, tokout)

    def sb(self, name, shape, dt, es=None, pool=None):
        if pool is None:
            pool = "f" if dt == F32 else "b"
        pl = self.pools[pool]
        n = 1
        for d in shape[1:]:
            n *= d
        n_al = (n + 1) // 2 * 2
        assert pl[1] + n_al <= pl[2], (name, pool, pl[1], n_al, pl[2])
        ap = pl[0][:, pl[1]:pl[1] + n]
        pl[1] += n_al
        if len(shape) == 3:
            ap = ap.rearrange("p (c t) -> p c t", c=shape[1])
        self.uid += 1
        self.lay.append((name, pool, pl[1] - n_al, n))
        return ap, Res("%s_%d" % (name, self.uid))

    def mark(self):
        self.keep = {k: v[1] for k, v in self.pools.items()}

    def reset(self, *pools):
        for p in pools:
            self.pools[p][1] = self.keep.get(p, 0)


def _rms_tile(C, x_t, x_r, g_t, g_r, ones_t, ones_r, sq, rstd, lnv, hT, TT, eps_t):
    P = C.P
    sq_t, sq_r = sq
    rs_t, rs_r = rstd
    ln_t, ln_r = lnv
    h_t, h_r = hT
    P.op("act", lambda e: e.activation(out=sq_t[:, :, :], in_=x_t[:, :, :], func=AF.Square),
         reads=[x_r], writes=[sq_r])
    bk, bk_r = C.bank()

    def mm(e):
        ins = None
        for c in range(8):
            ins = e.matmul(bk[:, 0:TT], lhsT=ones_t[:, :], rhs=sq_t[:, c, :], start=(c == 0), stop=(c == 7))
        return ins
    P.op("pe", mm, reads=[sq_r, ones_r], writes=[bk_r])
    eps_t, eps_r = eps_t
    P.op("act", lambda e: e.activation(out=ln_t[:, :], in_=bk[:, 0:TT], func=AF.Ln, scale=1.0 / D, bias=eps_t[:, 0:1]),
         reads=[bk_r, eps_r], writes=[ln_r])
    P.op("act", lambda e: e.activation(out=rs_t[:, :], in_=ln_t[:, :], func=AF.Exp, scale=-0.5),
         reads=[ln_r], writes=[rs_r])

    def hh(e):
        ins = None
        for c in range(8):
            ins = e.scalar_tensor_tensor(out=h_t[:, c, :], in0=x_t[:, c, :], scalar=g_t[:, c:c + 1], in1=rs_t[:, :],
                                         op0=ALU.mult, op1=ALU.mult)
        return ins
    P.op("dve", hh, reads=[x_r, g_r, rs_r], writes=[h_r])


def build_phase1(nc, C, dr):
    P = C.P
    es1 = None
    TT = 256
    NT = S // TT

    qT, qT_r = C.sb("qT", [128, 2, S], BF16, pool="r")
    kT, kT_r = C.sb("kT", [128, 2, S], BF16, pool="r")
    vS, vS_r = C.sb("vS", [128, 64, 256], BF16, pool="r")
    ones, ones_r = C.sb("ones", [128, 128], BF16)
    eps_t, eps_r = C.sb("eps", [128, 1], F32)
    one_t, one_r = C.sb("one1", [128, 1], F32)
    cst, cst_r = C.sb("cstb", [128, 2432], BF16)
    C.mark()

    w1b, w1b_r = C.sb("w1b", [128, 8, 1280], BF16, es1)
    wst = [C.sb("wst%d" % i, [128, 1280], F32, es1) for i in range(2)]
    gm, gm_r = C.sb("gm", [128, 8], F32, es1)
    vec, vec_r = C.sb("vec", [128, 16], F32, es1)
    dv, dv_r = C.sb("dv", [128, 8], F32, es1)
    bdst, bdst_r = C.sb("bdst", [128, 4, 128], F32, es1)
    bd, bd_r = C.sb("bd", [128, 4, 128], BF16, es1)

    P.op("pool", lambda e: e.memset(ones[:, :], 1.0), writes=[ones_r])
    P.op("pool", lambda e: e.memset(eps_t[:, :], EPS), writes=[eps_r])
    P.op("pool", lambda e: e.memset(one_t[:, :], 1.0), writes=[one_r])
    P.op("pool", lambda e: e.memset(bdst[:, :, :], 0.0), writes=[bdst_r])
    P.op("sp", lambda e: e.dma_start(out=gm[:, :], in_=dr["gm"][:, :]), writes=[gm_r], dma="gm")
    P.op("sp", lambda e: e.dma_start(out=vec[:, :], in_=dr["vec1"][:, :]), writes=[vec_r], dma="vec")
    for j in range(4):
        ct, hb = j // 2, j % 2
        for nm, base in (("wr", 0), ("wi", 2)):
            P.op("sp", lambda e, nm=nm, j=j, ct=ct, hb=hb, base=base: e.dma_start(
                out=bdst[64 * hb:64 * hb + 64, base + ct, 64 * hb:64 * hb + 64], in_=dr[nm][j, :, :]),
                writes=[bdst_r], dma="bdst")
    P.op("dve", lambda e: e.tensor_copy(out=bd[:, :, :], in_=bdst[:, :, :]), reads=[bdst_r], writes=[bd_r])
    for i in range(2):
        st_t, st_r = wst[i]
        P.op("sp", lambda e, i=i, st_t=st_t: e.dma_start(out=st_t[:, 0:1216], in_=dr["cst"][:, 1216 * i:1216 * (i + 1)]),
             writes=[st_r], dma="wst%d" % i)
        P.op("dve", lambda e, i=i, st_t=st_t: e.tensor_copy(out=cst[:, 1216 * i:1216 * (i + 1)], in_=st_t[:, 0:1216]),
             reads=[st_r], writes=[cst_r])
    for c in range(8):
        st_t, st_r = wst[c % 2]
        P.op("sp", lambda e, c=c, st_t=st_t: e.dma_start(out=st_t[:, :], in_=dr["w1"][128 * c:128 * (c + 1), :]),
             writes=[st_r], dma="wst%d" % (c % 2))
        P.op("pool" if c % 2 else "dve", lambda e, c=c, st_t=st_t: e.tensor_copy(out=w1b[:, c, :], in_=st_t[:, :]),
             reads=[st_r], writes=[w1b_r])
    tmpv, tmpv_r = C.sb("tmpv", [128, 4], F32, es1)
    for ct in range(2):
        P.op("act", lambda e, ct=ct: e.activation(out=tmpv[:, ct:ct + 1], in_=vec[:, 8 * ct + 7:8 * ct + 8], func=AF.Exp, scale=-1.0),
             reads=[vec_r], writes=[tmpv_r])
        P.op("act", lambda e, ct=ct: e.activation(out=tmpv[:, 2 + ct:3 + ct], in_=tmpv[:, ct:ct + 1], func=AF.Ln, bias=one_t[:, 0:1]),
             reads=[tmpv_r, one_r], writes=[tmpv_r])
        P.op("dve", lambda e, ct=ct: e.tensor_scalar(out=dv[:, 4 * ct:4 * ct + 1], in0=tmpv[:, 2 + ct:3 + ct], scalar1=-8.0, scalar2=None, op0=ALU.mult),
             reads=[tmpv_r], writes=[dv_r])
        P.op("dve", lambda e, ct=ct: e.tensor_scalar(out=dv[:, 4 * ct + 1:4 * ct + 2], in0=tmpv[:, 2 + ct:3 + ct], scalar1=-16.0, scalar2=None, op0=ALU.mult),
             reads=[tmpv_r], writes=[dv_r])
        P.op("dve", lambda e, ct=ct: e.tensor_scalar(out=dv[:, 4 * ct + 2:4 * ct + 4], in0=vec[:, 8 * ct + 5:8 * ct + 7], scalar1=-1.0, scalar2=None, op0=ALU.mult),
             reads=[vec_r], writes=[dv_r])

    xs = [C.sb("xs%d" % i, [128, 8, TT], F32, es1) for i in range(2)]
    sq = C.sb("sq", [128, 8, TT], BF16, es1)
    rstd = C.sb("rstd", [128, TT], F32, es1)
    lnv = C.sb("lnv", [128, TT], F32, es1)
    hTs = [C.sb("hT%d" % i, [128, 8, TT], BF16, es1) for i in range(2)]
    uxb = [[C.sb("ux%d_%d" % (ct, i), [128, TT + 4], F32, es1) for i in range(2)] for ct in range(2)]
    ugb = [[C.sb("ug%d_%d" % (ct, i), [128, TT], F32, es1) for i in range(2)] for ct in range(2)]
    T = {}
    for nm in ("c", "r", "i", "a", "a2", "m", "u", "w", "e", "gg"):
        for ct in range(2):
            T[nm, ct] = C.sb("t_%s%d" % (nm, ct), [128, TT], F32, es1)
    rscr = C.sb("rscr", [128, TT], F32, es1)
    cbf = [C.sb("cbf%d" % ct, [128, TT], BF16, es1) for ct in range(2)]
    hh = [[C.sb("h%d_%d" % (ct, i), [128, TT], F32, es1) for i in range(2)] for ct in range(2)]
    ysb = [[C.sb("y%d_%d" % (ct, i), [128, TT], BF16, es1) for i in range(2)] for ct in range(2)]
    for ct in range(2):
        P.op("pool", lambda e, ct=ct: e.memset(uxb[ct][0][0][:, 0:3], 0.0), writes=[uxb[ct][0][1]])

    xTv = dr["xT"].rearrange("(c p) t -> p c t", p=128)

    def load_x(tt):
        x_t, x_r = xs[tt % 2]
        P.op("sp", lambda e: e.dma_start(out=x_t[:, :, :], in_=xTv[:, :, tt * TT:(tt + 1) * TT]),
             writes=[x_r], dma="xs%d" % (tt % 2))

    def tile(tt):
        t0 = tt * TT
        if tt + 1 < NT:
            load_x(tt + 1)
        x_t, x_r = xs[tt % 2]
        h_t, h_r = hTs[tt % 2]
        ux = [uxb[ct][tt % 2] for ct in range(2)]
        ug = [ugb[ct][tt % 2] for ct in range(2)]
        _rms_tile(C, x_t, x_r, gm, gm_r, ones, ones_r, sq, rstd, lnv, (h_t, h_r), TT, (eps_t, eps_r))
        if P.cap is not None:
            split.append(len(P.cap))
        for n in range(8):
            bk, bk_r = C.bank()

            def mm(e, n=n, bk=bk):
                ins = None
                for c in range(8):
                    ins = e.matmul(bk[:, 0:TT], lhsT=w1b[:, c, 128 * n:128 * (n + 1)], rhs=h_t[:, c, :],
                                   start=(c == 0), stop=(c == 7))
                return ins
            P.op("pe", mm, reads=[w1b_r, h_r], writes=[bk_r])
            if n < 2:
                P.op("dve", lambda e, n=n, bk=bk: e.tensor_copy(out=ux[n][0][:, 3:3 + TT], in_=bk[:, 0:TT]),
                     reads=[bk_r], writes=[ux[n][1]])
                if tt > 0:
                    up_t, up_r = uxb[n][(tt + 1) % 2]
                    P.op("pool", lambda e, n=n, up_t=up_t: e.tensor_copy(out=ux[n][0][:, 0:3], in_=up_t[:, TT:TT + 3]),
                         reads=[up_r], writes=[ux[n][1]])
            elif n < 4:
                P.op("dve", lambda e, n=n, bk=bk: e.tensor_copy(out=ug[n - 2][0][:, :], in_=bk[:, 0:TT]),
                     reads=[bk_r], writes=[ug[n - 2][1]])
            elif n < 6:
                P.op("act", lambda e, n=n, bk=bk: e.activation(out=qT[:, n - 4, t0:t0 + TT], in_=bk[:, 0:TT], func=AF.Copy, scale=0.125),
                     reads=[bk_r], writes=[qT_r])
            else:
                P.op("act", lambda e, n=n, bk=bk: e.activation(out=kT[:, n - 6, t0:t0 + TT], in_=bk[:, 0:TT], func=AF.Copy),
                     reads=[bk_r], writes=[kT_r])
        for sub in range(TT // 128):
            bk, bk_r = C.bank()
            kb = (t0 // 128) + sub

            def mmv(e, sub=sub, bk=bk):
                ins = None
                for c in range(8):
                    ins = e.matmul(bk[:, 0:256], lhsT=h_t[:, c, 128 * sub:128 * (sub + 1)], rhs=w1b[:, c, 1024:1280],
                                   start=(c == 0), stop=(c == 7))
                return ins
            P.op("pe", mmv, reads=[w1b_r, h_r], writes=[bk_r])
            P.op("dve", lambda e, kb=kb, bk=bk: e.tensor_copy(out=vS[:, kb, :], in_=bk[:, 0:256]),
                 reads=[bk_r], writes=[vS_r])
        if P.cap is not None:
            split.append(len(P.cap))
        def TT_(nm, ct):
            return T[nm, ct]
        for k in (3, 2, 1, 0):
            for ct in range(2):
                c_t, c_r = T["c", ct]
                u_t, u_r = ux[ct]
                if k == 3:
                    P.op("dve", lambda e, ct=ct, c_t=c_t, u_t=u_t: e.tensor_scalar(
                        out=c_t[:, :], in0=u_t[:, 3:3 + TT], scalar1=vec[:, 8 * ct + 3:8 * ct + 4], scalar2=vec[:, 8 * ct + 4:8 * ct + 5],
                        op0=ALU.mult, op1=ALU.add), reads=[u_r, vec_r], writes=[c_r])
                else:
                    P.op("dve", lambda e, ct=ct, k=k, c_t=c_t, u_t=u_t: e.scalar_tensor_tensor(
                        out=c_t[:, :], in0=u_t[:, k:k + TT], scalar=vec[:, 8 * ct + k:8 * ct + k + 1], in1=c_t[:, :],
                        op0=ALU.mult, op1=ALU.add), reads=[u_r, vec_r, c_r], writes=[c_r])
        for ct in range(2):
            u_t, u_r = ux[ct]
            P.op("pool", lambda e, ct=ct: e.tensor_copy(out=cbf[ct][0][:, :], in_=T["c", ct][0][:, :]),
                 reads=[T["c", ct][1]], writes=[cbf[ct][1]])
        if P.cap is not None:
            split.append(len(P.cap))
        gates = []
        gbanks = []
        for ct in range(2):
            for gi, nm in ((0, "r"), (1, "i")):
                bk, bk_r = C.bank(kind="g")
                P.op("pe", lambda e, ct=ct, gi=gi, bk=bk: e.matmul(bk[:, 0:TT], lhsT=bd[:, 2 * gi + ct, :], rhs=cbf[ct][0][:, :], start=True, stop=True),
                     reads=[bd_r, cbf[ct][1]], writes=[bk_r])
                gbanks.append((ct, gi, nm, bk, bk_r))
        if P.cap is not None:
            split.append(len(P.cap))
        for ct, gi, nm, bk, bk_r in gbanks:
            o_t, o_r = T[nm, ct]
            P.op("act", lambda e, ct=ct, gi=gi, bk=bk, o_t=o_t: e.activation(
                out=o_t[:, :], in_=bk[:, 0:TT], func=AF.Exp, scale=-1.0, bias=dv[:, 4 * ct + 2 + gi:4 * ct + 3 + gi]),
                reads=[bk_r, dv_r], writes=[o_r])
            gates.append((o_t, o_r))
        for o_t, o_r in gates:
            P.op("act", lambda e, o_t=o_t: e.activation(out=o_t[:, :], in_=o_t[:, :], func=AF.Ln, bias=1.0),
                 reads=[o_r], writes=[o_r])
        for o_t, o_r in gates:
            P.op("act", lambda e, o_t=o_t: e.activation(out=o_t[:, :], in_=o_t[:, :], func=AF.Exp, scale=-1.0),
                 reads=[o_r], writes=[o_r])
        for ct in range(2):
            r_t, r_r = T["r", ct]
            P.op("act", lambda e, ct=ct, r_t=r_t: e.activation(out=T["a", ct][0][:, :], in_=r_t[:, :], func=AF.Exp, scale=dv[:, 4 * ct:4 * ct + 1]),
                 reads=[r_r, dv_r], writes=[T["a", ct][1]])
            P.op("act", lambda e, ct=ct, r_t=r_t: e.activation(out=T["a2", ct][0][:, :], in_=r_t[:, :], func=AF.Exp, scale=dv[:, 4 * ct + 1:4 * ct + 2]),
                 reads=[r_r, dv_r], writes=[T["a2", ct][1]])
        for ct in range(2):
            P.op("dve", lambda e, ct=ct: e.tensor_scalar(out=T["a2", ct][0][:, :], in0=T["a2", ct][0][:, :], scalar1=-1.0, scalar2=1.0, op0=ALU.mult, op1=ALU.add),
                 reads=[T["a2", ct][1]], writes=[T["a2", ct][1]])
        for ct in range(2):
            P.op("act", lambda e, ct=ct: e.activation(out=T["m", ct][0][:, :], in_=T["a2", ct][0][:, :], func=AF.Ln),
                 reads=[T["a2", ct][1]], writes=[T["m", ct][1]])
        for ct in range(2):
            P.op("act", lambda e, ct=ct: e.activation(out=T["m", ct][0][:, :], in_=T["m", ct][0][:, :], func=AF.Exp, scale=0.5),
                 reads=[T["m", ct][1]], writes=[T["m", ct][1]])
        for ct in range(2):
            g_t, g_r = ug[ct]
            w_t, w_r = T["w", ct]
            P.op("pool", lambda e, g_t=g_t, w_t=w_t: e.tensor_tensor(out=w_t[:, :], in0=g_t[:, :], in1=g_t[:, :], op=ALU.mult),
                 reads=[g_r], writes=[w_r])
        for ct in range(2):
            w_t, w_r = T["w", ct]
            P.op("pool", lambda e, w_t=w_t: e.tensor_scalar(out=w_t[:, :], in0=w_t[:, :], scalar1=0.044715, scalar2=1.0, op0=ALU.mult, op1=ALU.add),
                 reads=[w_r], writes=[w_r])
        for ct in range(2):
            g_t, g_r = ug[ct]
            w_t, w_r = T["w", ct]
            P.op("pool", lambda e, g_t=g_t, w_t=w_t: e.tensor_tensor(out=w_t[:, :], in0=w_t[:, :], in1=g_t[:, :], op=ALU.mult),
                 reads=[w_r, g_r], writes=[w_r])
        for ct in range(2):
            P.op("act", lambda e, ct=ct: e.activation(out=T["e", ct][0][:, :], in_=T["w", ct][0][:, :], func=AF.Exp, scale=-GELU_K),
                 reads=[T["w", ct][1]], writes=[T["e", ct][1]])
        for ct in range(2):
            P.op("pool", lambda e, ct=ct: e.tensor_tensor(out=T["u", ct][0][:, :], in0=T["m", ct][0][:, :], in1=T["i", ct][0][:, :], op=ALU.mult),
                 reads=[T["m", ct][1], T["i", ct][1]], writes=[T["u", ct][1]])
        for ct in range(2):
            P.op("pool", lambda e, ct=ct: e.tensor_tensor(out=T["u", ct][0][:, :], in0=T["u", ct][0][:, :], in1=T["c", ct][0][:, :], op=ALU.mult),
                 reads=[T["u", ct][1], T["c", ct][1]], writes=[T["u", ct][1]])
        if P.cap is not None:
            split.append(len(P.cap))
        for ct in range(2):
            hc_t, hc_r = hh[ct][tt % 2]
            hp_t, hp_r = hh[ct][(tt + 1) % 2]
            if tt == 0:
                P.op("dve", lambda e, ct=ct, hc_t=hc_t: e.tensor_tensor_scan(
                    out=hc_t[:, :], data0=T["a", ct][0][:, :], data1=T["u", ct][0][:, :], initial=0.0, op0=ALU.mult, op1=ALU.add),
                    reads=[T["a", ct][1], T["u", ct][1]], writes=[hc_r])
            else:
                P.op("dve", lambda e, ct=ct, hc_t=hc_t, hp_t=hp_t: e.tensor_tensor_scan(
                    out=hc_t[:, :], data0=T["a", ct][0][:, :], data1=T["u", ct][0][:, :], initial=hp_t[:, TT - 1:TT], op0=ALU.mult, op1=ALU.add),
                    reads=[T["a", ct][1], T["u", ct][1], hp_r], writes=[hc_r])
        for ct in range(2):
            P.op("act", lambda e, ct=ct: e.activation(out=T["e", ct][0][:, :], in_=T["e", ct][0][:, :], func=AF.Ln, bias=1.0),
                 reads=[T["e", ct][1]], writes=[T["e", ct][1]])
        for ct in range(2):
            P.op("act", lambda e, ct=ct: e.activation(out=T["e", ct][0][:, :], in_=T["e", ct][0][:, :], func=AF.Exp, scale=-1.0),
                 reads=[T["e", ct][1]], writes=[T["e", ct][1]])
        for ct in range(2):
            P.op("pool", lambda e, ct=ct: e.tensor_tensor(out=T["gg", ct][0][:, :], in0=T["e", ct][0][:, :], in1=ug[ct][0][:, :], op=ALU.mult),
                 reads=[T["e", ct][1], ug[ct][1]], writes=[T["gg", ct][1]])
        for ct in range(2):
            y_t, y_r = ysb[ct][tt % 2]
            hc_t, hc_r = hh[ct][tt % 2]
            P.op("pool", lambda e, ct=ct, y_t=y_t, hc_t=hc_t: e.tensor_tensor(out=y_t[:, :], in0=T["gg", ct][0][:, :], in1=hc_t[:, :], op=ALU.mult),
                 reads=[T["gg", ct][1], hc_r], writes=[y_r])
            jj, co = t0 // 1024, t0 % 1024
            P.op("sp", lambda e, ct=ct, y_t=y_t, co=co, jj=jj: e.dma_start(out=dr["yin"][jj][128 * ct:128 * (ct + 1), co:co + TT], in_=y_t[:, :]),
                 reads=[y_r], dma="y%d_%d" % (ct, tt % 2), tokout=dr["yin_tok"][jj])

    pcl = []
    if "wsc_r" in dr:
        for nm, K_, N_ in WSPEC:
            for c in range(K_ // 128):
                pcl.append((nm, c))

    def precast(i):
        nm, c = pcl[i]
        P.op("pool", lambda e: e.dma_start(out=dr[nm + "_b"][128 * c:128 * (c + 1), :], in_=dr[nm][128 * c:128 * (c + 1), :],
                                           max_dma_last_dim=4096),
             writes=[dr["wsc_r"][nm]], dma="pc_" + nm, nowaw=True)

    split = []
    load_x(0)
    npc = 0

    def cap_tile(tt):
        P.cap = []
        tile(tt)
        ops, P.cap = P.cap, None
        m5 = split.pop()
        m4 = split.pop()
        m3 = split.pop()
        m2 = split.pop()
        m1 = split.pop()
        return ops[:m1], ops[m1:m2], ops[m2:m3], ops[m4:m5], ops[m5:], ops[m3:m4]

    def mix(a, b):
        out = []
        ia = ib = 0
        na, nb_ = len(a), len(b)
        while ia < na or ib < nb_:
            if ib >= nb_ or (ia < na and ia * nb_ <= ib * na):
                out.append(a[ia])
                ia += 1
            else:
                out.append(b[ib])
                ib += 1
        return out

    parts = {}
    parts[0] = cap_tile(0)
    C.replay(parts[0][0])
    C.replay(parts[0][1])
    C.replay(parts[0][2])
    parts[1] = cap_tile(1)
    C.replay(parts[1][0])
    for tt in range(NT):
        if tt + 2 < NT:
            parts[tt + 2] = cap_tile(tt + 2)
            C.replay(parts[tt + 2][0])
        C.replay(parts[tt][5])
        nxtB = parts[tt + 1][1] if tt + 1 < NT else []
        C.replay(mix(nxtB, parts[tt][3]))
        if tt + 1 < NT:
            C.replay(parts[tt + 1][2])
        C.replay(parts[tt][4])
        del parts[tt]
    dr["precast_fn"] = precast
    dr["precast_n"] = len(pcl)
    P.barrier()
    if "dbgF" in dr:
        C.layout = {"vec": 0}
        P.op("sp", lambda e: e.dma_start(out=dr["dbgF"][:, :], in_=C.pools["f"][0][:, :]), dma="dbgF")
        P.op("sp", lambda e: e.dma_start(out=dr["dbgB"][:, :], in_=C.pools["b"][0][:, :]), dma="dbgB")
        P.op("sp", lambda e: e.dma_start(out=dr["dbgR"][:, :], in_=C.pools["r"][0][:, :]), dma="dbgR")
        P.barrier()
        return
    C.reset("f", "b")
    build_attention(nc, C, dr, qT, qT_r, kT, kT_r, vS, vS_r, cst, cst_r, one_t, one_r)


def build_attention(nc, C, dr, qT, qT_r, kT, kT_r, vS, vS_r, cst, cst_r, one_t, one_r):
    P = C.P
    ZB = [(C.ps[:, 1024 * k:1024 * (k + 1)], Res("zb%d" % k)) for k in range(3)]
    O_b = [(C.ps[:, 3072 + 512 * h:3072 + 512 * (h + 1)], Res("ob%d" % h)) for h in range(2)]
    E_s = [C.sb("E%d" % i, [128, 1024], F32) for i in range(2)]
    L_s = [C.sb("Lp%d" % i, [128, 1024], BF16) for i in range(3)]
    A_s = [C.sb("As%d" % i, [128, 1024], BF16) for i in range(3)]
    Sacc = C.sb("Sacc", [128, 1024], F32)
    Sbf = [C.sb("Sbf%d" % j, [128, 1024], BF16) for j in range(2)]
    yst = [C.sb("yst%d" % i, [128, 512], BF16) for i in range(4)]
    triN = cst[:, 0:128]
    onesN = cst[:, 128:256]

    ident = cst[:, 256:384]

    def negmask(j):
        return cst[:, 384 + 512 * j:384 + 512 * (j + 1)]

    steps = []
    for qt in range(S // 512):
        for pair in range(2):
            for kb in range(4 * qt + 3, -1, -1):
                steps.append((pair, qt, kb))
    NB = len(steps)
    nyst = [0]
    sprev = {}
    ns = 0
    for i, (pair, qt, kb) in enumerate(steps):
        first = (kb == 4 * qt + 3)
        second = (kb == 4 * qt + 2)
        if second:
            sprev[i] = L_s[(i - 1) % 3]
        elif not first:
            sprev[i] = Sbf[(ns - 1) % 2]
        if (not first) and kb > 0:
            ns += 1
    nsw = [0]

    def c0of(qt, kb):
        jd = kb - 4 * qt
        return 128 * jd if jd > 0 else 0

    def v3(ap, c0):
        return ap.rearrange("p (h c) -> p h c", h=2)[:, :, c0:512]

    def pe1(i):
        pair, qt, kb = steps[i]
        z_t, z_r = ZB[i % 3]
        jd = kb - 4 * qt
        c0 = c0of(qt, kb)

        def f(e):
            ins = None
            for hs in range(2):
                lo = 64 * hs
                ins = e.matmul(z_t[:, 512 * hs + c0:512 * (hs + 1)], lhsT=kT[lo:lo + 64, pair, 128 * kb:128 * (kb + 1)],
                               rhs=qT[lo:lo + 64, pair, 512 * qt + c0:512 * (qt + 1)], start=True, stop=False, skip_group_check=True)
                if jd >= 0:
                    ins = e.matmul(z_t[:, 512 * hs + c0:512 * (hs + 1)], lhsT=ident, rhs=negmask(jd)[:, c0:512], start=False, stop=False, skip_group_check=True)
            return ins
        P.op("pe", f, reads=[kT_r, qT_r, cst_r], writes=[z_r])

    def act12(i):
        pair, qt, kb = steps[i]
        z_t, z_r = ZB[i % 3]
        e_t, e_r = E_s[i % 2]
        l_t, l_r = L_s[i % 3]
        c0 = c0of(qt, kb)
        if c0 == 0:
            zz, ee, ll = z_t[:, :], e_t[:, :], l_t[:, :]
        else:
            zz, ee, ll = v3(z_t, c0), v3(e_t, c0), v3(l_t, c0)
        P.op("act", lambda e: e.activation(out=ee, in_=zz, func=AF.Exp), reads=[z_r], writes=[e_r])
        P.op("act", lambda e: e.activation(out=ll, in_=ee, func=AF.Ln, bias=1.0), reads=[e_r], writes=[l_r])
        first = (kb == 4 * qt + 3)
        if kb > 0:
            s_t, s_r = Sacc
            ss = s_t[:, :] if c0 == 0 else v3(s_t, c0)
            if first:
                P.op("pool", lambda e: e.memset(s_t[:, :], 0.0), writes=[s_r])
                P.op("dve", lambda e: e.tensor_copy(out=ss, in_=ll), reads=[l_r], writes=[s_r])
            else:
                P.op("dve", lambda e: e.tensor_tensor(out=ss, in0=ss, in1=ll, op=ALU.add),
                     reads=[s_r, l_r], writes=[s_r])
                sb_t, sb_r = Sbf[nsw[0] % 2]
                nsw[0] += 1
                P.op("dve", lambda e: e.tensor_copy(out=sb_t[:, :], in_=s_t[:, :]), reads=[s_r], writes=[sb_r])

    def pe2(i):
        pair, qt, kb = steps[i]
        z_t, z_r = ZB[i % 3]
        l_t, l_r = L_s[i % 3]
        first = (kb == 4 * qt + 3)
        second = (kb == 4 * qt + 2)
        c0 = c0of(qt, kb)
        cc = 384 if second else c0
        rd = [l_r, cst_r, z_r]
        if not first:
            sb_t, sb_r = sprev[i]
            rd.append(sb_r)

        def f(e):
            ins = None
            for hs in range(2):
                cs = slice(512 * hs + c0, 512 * (hs + 1))
                ins = e.matmul(z_t[:, cs], lhsT=triN, rhs=l_t[:, cs], start=False, stop=first, skip_group_check=True)
                if not first:
                    c2 = slice(512 * hs + cc, 512 * (hs + 1))
                    ins = e.matmul(z_t[:, c2], lhsT=onesN, rhs=sb_t[:, c2], start=False, stop=True, skip_group_check=True)
            return ins
        P.op("pe", f, reads=rd, writes=[z_r])

    def act3(i):
        pair, qt, kb = steps[i]
        z_t, z_r = ZB[i % 3]
        as_t, as_r = A_s[i % 3]
        c0 = c0of(qt, kb)
        if c0 == 0:
            zz, aa = z_t[:, :], as_t[:, :]
        else:
            zz, aa = v3(z_t, c0), v3(as_t, c0)
        P.op("act", lambda e: e.activation(out=aa, in_=zz, func=AF.Exp), reads=[z_r], writes=[as_r])

    def pe3(i):
        pair, qt, kb = steps[i]
        as_t, as_r = A_s[i % 3]
        first = (kb == 4 * qt + 3)
        last = (kb == 0)
        c0 = c0of(qt, kb)

        def f(e):
            ins = None
            for hs in range(2):
                head = 2 * pair + hs
                ins = e.matmul(O_b[hs][0][0:64, c0:512], lhsT=vS[:, kb, 64 * head:64 * (head + 1)], rhs=as_t[:, 512 * hs + c0:512 * (hs + 1)],
                               start=first, stop=last, skip_group_check=True)
            return ins
        P.op("pe", f, reads=[vS_r, as_r], writes=[O_b[0][1], O_b[1][1]])
        if last:
            jj, co = qt // 2, 512 * (qt % 2)
            for hs in range(2):
                head = 2 * pair + hs
                y_t, y_r = yst[nyst[0] % 4]
                nyst[0] += 1
                P.op("dve", lambda e, hs=hs, y_t=y_t: e.tensor_copy(out=y_t[0:64, :], in_=O_b[hs][0][0:64, :]), reads=[O_b[hs][1]], writes=[y_r])
                P.op("sp", lambda e, head=head, y_t=y_t: e.dma_start(
                    out=dr["yin"][jj][256 + 64 * head:256 + 64 * (head + 1), co:co + 512], in_=y_t[0:64, :]),
                    reads=[y_r], dma=y_r.name, tokout=dr["yin_tok"][jj])
            if pair == 1 and qt % 2 == 1:
                dr["issue_cc"](jj)

    npcs = dr.get("precast_n", 0)
    pcdone = 0
    for j in range(NB + 3):
        tgt = min(npcs, (npcs * (j + 1) * 5) // (NB * 4) + 1) if npcs else 0
        while pcdone < tgt:
            dr["precast_fn"](pcdone)
            pcdone += 1
        if j < NB:
            pe1(j)
        if 0 <= j - 2 < NB:
            pe2(j - 2)
        if 0 <= j - 3 < NB:
            pe3(j - 3)
        if 0 <= j - 1 < NB:
            act12(j - 1)
        if 0 <= j - 2 < NB:
            act3(j - 2)
    P.barrier()


WSPEC = (("wg", 1024, 2048), ("wbl", 1024, 1024), ("wba", 1024, 1024), ("wo", 1024, 1024), ("wup", 1024, 4096),
         ("wdn", 4096, 1024), ("wpg", 1024, 1024), ("wple", 256, 1024))


def build_phase2(nc, C, dr):
    P = C.P
    TT = 1024
    NS = TT // 512
    NT2 = 2048 // TT
    C.pools["f"][1] = 0
    C.pools["b"][1] = 0
    C.pools["r"][1] = 0
    ones, ones_r = C.sb("ones2", [128, 128], BF16)
    eps_t, eps_r = C.sb("eps2", [128, 1], F32)
    g2, g2_r = C.sb("g2", [128, 32], F32)
    P.op("pool", lambda e: e.memset(ones[:, :], 1.0), writes=[ones_r])
    P.op("pool", lambda e: e.memset(eps_t[:, :], EPS), writes=[eps_r])
    P.op("sp", lambda e: e.dma_start(out=g2[:, :], in_=dr["g2"][:, :]), writes=[g2_r], dma="g2")
    x_t, x_r = C.sb("x2", [128, 8, TT], F32)
    rstd = C.sb("rstd2", [128, 512], F32)
    lnv = C.sb("lnv2", [128, 512], F32)
    tA = C.sb("tA", [128, 4 * NS, 512], F32)
    tB = C.sb("tB", [128, 4 * NS, 512], F32)
    pst = (tA[0][:, 0:4, :].rearrange("p (c a) n -> p c (a n)", c=2), tA[1])
    hT = C.sb("h2T", [128, 8, TT], BF16)
    mg = C.sb("mg", [128, 8, TT], BF16)
    sq = mg
    pb = C.sb("pb", [128, 2, TT], BF16)
    act = C.sb("actT", [128, 32, TT], BF16, pool="r")
    yt = (act[0][:, 0:16, :], act[1])
    wb = [C.sb("wb%d" % i, [128, 8, 512], BF16, pool="r") for i in range(3)]
    wpl = C.sb("wpl", [128, 2, 1024], BF16, pool="r")
    ng = [0]

    def gran(wname, r0, c0, kc=8, gc=512, dst=None):
        i = ng[0]
        ng[0] += 1
        w_t, w_r = dst if dst is not None else wb[i % 3]
        src = dr[wname + "_b"][r0:r0 + 128 * kc, c0:c0 + gc].rearrange("(c p) n -> p c n", p=128)
        wv = w_t if dst is not None else w_t[:, :, 0:gc]
        P.op("sp", lambda e: e.dma_start(out=wv, in_=src), reads=[dr["wsc_r"][wname]], writes=[w_r], dma=w_r.name)
        return w_t, w_r

    def mmgroup(bk, bk_r, w_t, w_r, col, rhs_list, rhs_res, start=True, stop=True):
        n = len(rhs_list)

        def f(e):
            ins = None
            for c in range(n):
                ins = e.matmul(bk[:, :], lhsT=w_t[:, c, 128 * col:128 * (col + 1)], rhs=rhs_list[c],
                               start=(start and c == 0), stop=(stop and c == n - 1))
            return ins
        P.op("pe", f, reads=[w_r] + list(rhs_res), writes=[bk_r])

    xv = dr["xT2"].rearrange("(c p) t -> p c t", p=128)
    ov = dr["oT"].rearrange("(c p) t -> p c t", p=128)
    pv = dr["pT"].rearrange("(c p) t -> p c t", p=128)

    def sl(sub):
        return slice(512 * sub, 512 * (sub + 1))

    def tile2(tt):
        t0 = tt * TT
        y_t, y_r = yt
        P.op("sp", lambda e: e.dma_start(out=x_t[:, :, :], in_=xv[:, :, t0:t0 + TT]), writes=[x_r], dma=x_r.name)
        for r in range(4):
            def ld(e, r=r):
                if "rank" not in dr:
                    dr["rank"] = e.partition_id() % 4
                rank = dr["rank"]
                src = dr["yall"][bass.ds((rank * 2 + tt) * 2048 + 512 * r, 512), :].rearrange("(c p) t -> p c t", p=128)
                return e.dma_start(out=y_t[:, 4 * r:4 * r + 4, :], in_=src)
            P.op("pool", ld, reads=list(dr["yall_r"]), writes=[y_r], dma=y_r.name)
        P.op("sp", lambda e: e.dma_start(out=pst[0][:, :, :], in_=pv[:, :, t0:t0 + TT]), writes=[pst[1]], dma=pst[1].name)
        P.op("pool", lambda e: e.tensor_copy(out=pb[0][:, :, :], in_=pst[0][:, :, :]), reads=[pst[1]], writes=[pb[1]])

        def ylru(sub):
            return [y_t[:, 4 * (c // 2) + (c % 2), sl(sub)] for c in range(8)]

        def yatt(sub):
            return [y_t[:, 4 * (c // 2) + 2 + (c % 2), sl(sub)] for c in range(8)]

        def hl(sub):
            return [hT[0][:, c, sl(sub)] for c in range(8)]

        def rms(gcol, out):
            for sub in range(NS):
                _rms_tile(C, x_t[:, :, sl(sub)], x_r, g2[:, gcol:gcol + 8], g2_r, ones, ones_r,
                          (sq[0][:, :, sl(sub)], sq[1]), rstd, lnv,
                          (out[0][:, :, sl(sub)], out[1]), 512, (eps_t, eps_r))

        rms(0, hT)
        for cg in range(2):
            w_t, w_r = gran("wg", 0, 512 * cg)
            for t in range(4):
                for sub in range(NS):
                    bk, bk_r = C.bank()
                    mmgroup(bk, bk_r, w_t, w_r, t, hl(sub), [hT[1]])
                    P.op("act", lambda e, t=t, sub=sub, bk=bk: e.activation(out=tA[0][:, NS * t + sub, :], in_=bk[:, :], func=AF.Sigmoid),
                         reads=[bk_r], writes=[tA[1]])
            w_t, w_r = gran("wbl", 0, 512 * cg)
            for t in range(4):
                for sub in range(NS):
                    bk, bk_r = C.bank()
                    mmgroup(bk, bk_r, w_t, w_r, t, ylru(sub), [y_r])
                    P.op("dve", lambda e, t=t, sub=sub, bk=bk: e.tensor_tensor(out=tA[0][:, NS * t + sub, :], in0=tA[0][:, NS * t + sub, :], in1=bk[:, :], op=ALU.mult),
                         reads=[bk_r, tA[1]], writes=[tA[1]])
            w_t, w_r = gran("wg", 0, 1024 + 512 * cg)
            for t in range(4):
                for sub in range(NS):
                    bk, bk_r = C.bank()
                    mmgroup(bk, bk_r, w_t, w_r, t, hl(sub), [hT[1]])
                    P.op("act", lambda e, t=t, sub=sub, bk=bk: e.activation(out=tB[0][:, NS * t + sub, :], in_=bk[:, :], func=AF.Sigmoid),
                         reads=[bk_r], writes=[tB[1]])
            w_t, w_r = gran("wba", 0, 512 * cg)
            for t in range(4):
                for sub in range(NS):
                    bk, bk_r = C.bank()
                    mmgroup(bk, bk_r, w_t, w_r, t, yatt(sub), [y_r])
                    P.op("dve", lambda e, t=t, sub=sub, bk=bk: e.tensor_tensor(out=tB[0][:, NS * t + sub, :], in0=tB[0][:, NS * t + sub, :], in1=bk[:, :], op=ALU.mult),
                         reads=[bk_r, tB[1]], writes=[tB[1]])
            for t in range(4):
                P.op("pool", lambda e, cg=cg, t=t: e.tensor_tensor(
                    out=mg[0][:, 4 * cg + t, :], in0=tA[0][:, NS * t:NS * (t + 1), :].rearrange("p s n -> p (s n)"),
                    in1=tB[0][:, NS * t:NS * (t + 1), :].rearrange("p s n -> p (s n)"), op=ALU.add),
                    reads=[tA[1], tB[1]], writes=[mg[1]])
        for cg in range(2):
            w_t, w_r = gran("wo", 0, 512 * cg)
            for t in range(4):
                for sub in range(NS):
                    bk, bk_r = C.bank()
                    mmgroup(bk, bk_r, w_t, w_r, t, [mg[0][:, c, sl(sub)] for c in range(8)], [mg[1]])
                    P.op("dve", lambda e, t=t, cg=cg, sub=sub, bk=bk: e.tensor_tensor(
                        out=x_t[:, 4 * cg + t, sl(sub)], in0=x_t[:, 4 * cg + t, sl(sub)], in1=bk[:, :], op=ALU.add),
                        reads=[bk_r, x_r], writes=[x_r])
        rms(8, hT)
        for cg in range(8):
            w_t, w_r = gran("wup", 0, 512 * cg)
            for t in range(4):
                for sub in range(NS):
                    bk, bk_r = C.bank()
                    mmgroup(bk, bk_r, w_t, w_r, t, hl(sub), [hT[1]])
                    tmp = tA if (t + sub) % 2 == 0 else tB
                    P.op("act", lambda e, bk=bk, tmp=tmp, t=t, sub=sub: e.activation(out=tmp[0][:, NS * t + sub, :], in_=bk[:, :], func=AF.Relu),
                         reads=[bk_r], writes=[tmp[1]])
                    P.op("pool" if (t + sub) % 2 else "dve", lambda e, t=t, cg=cg, tmp=tmp, sub=sub: e.tensor_tensor(
                        out=act[0][:, 4 * cg + t, sl(sub)], in0=tmp[0][:, NS * t + sub, :], in1=tmp[0][:, NS * t + sub, :], op=ALU.mult),
                        reads=[tmp[1]], writes=[act[1]])
        for cg in range(4):
            bks = [[C.bank() for sub in range(NS)] for t in range(2)]
            for kg in range(4):
                w_t, w_r = gran("wdn", 1024 * kg, 256 * cg, gc=256)
                for t in range(2):
                    for sub in range(NS):
                        al = [act[0][:, 8 * kg + c, sl(sub)] for c in range(8)]
                        mmgroup(bks[t][sub][0], bks[t][sub][1], w_t, w_r, t, al, [act[1]], start=(kg == 0), stop=(kg == 3))
            for t in range(2):
                for sub in range(NS):
                    P.op("dve", lambda e, t=t, cg=cg, sub=sub, bk=bks[t][sub][0]: e.tensor_tensor(
                        out=x_t[:, 2 * cg + t, sl(sub)], in0=x_t[:, 2 * cg + t, sl(sub)], in1=bk[:, :], op=ALU.add),
                        reads=[bks[t][sub][1], x_r], writes=[x_r])
        rms(16, hT)
        wp_t, wp_r = gran("wple", 0, 0, kc=2, gc=1024, dst=wpl)
        for cg in range(2):
            w_t, w_r = gran("wpg", 0, 512 * cg)
            for t in range(4):
                for sub in range(NS):
                    bk, bk_r = C.bank()
                    mmgroup(bk, bk_r, w_t, w_r, t, hl(sub), [hT[1]])
                    P.op("act", lambda e, t=t, sub=sub, bk=bk: e.activation(out=tA[0][:, NS * t + sub, :], in_=bk[:, :], func=AF.Sigmoid),
                         reads=[bk_r], writes=[tA[1]])
            for t in range(4):
                for sub in range(NS):
                    bk, bk_r = C.bank()
                    mmgroup(bk, bk_r, wp_t, wp_r, 4 * cg + t, [pb[0][:, c, sl(sub)] for c in range(2)], [pb[1]])
                    P.op("dve", lambda e, t=t, sub=sub, bk=bk: e.tensor_tensor(out=tA[0][:, NS * t + sub, :], in0=tA[0][:, NS * t + sub, :], in1=bk[:, :], op=ALU.mult),
                         reads=[bk_r, tA[1]], writes=[tA[1]])
            for t in range(4):
                P.op("dve", lambda e, cg=cg, t=t: e.tensor_tensor(
                    out=x_t[:, 4 * cg + t, :], in0=x_t[:, 4 * cg + t, :], in1=tA[0][:, NS * t:NS * (t + 1), :].rearrange("p s n -> p (s n)"), op=ALU.add),
                    reads=[tA[1], x_r], writes=[x_r])
        rms(24, (x_t, x_r))
        P.op("sp", lambda e: e.dma_start(out=ov[:, :, t0:t0 + TT], in_=x_t[:, :, :]), reads=[x_r], dma="outst")

    for tt in range(NT2):
        tile2(tt)
    P.barrier()


def _consts():
    c = np.zeros((128, 2432), np.float32)
    jj = np.arange(128)[:, None]
    ss = np.arange(128)[None, :]
    c[:, 0:128] = -(jj >= ss).astype(np.float32)
    c[:, 128:256] = -1.0
    c[:, 256:384] = np.eye(128, dtype=np.float32)
    cc = np.arange(512)[None, :]
    for j in range(4):
        c[:, 384 + 512 * j:384 + 512 * (j + 1)] = -30000.0 * ((128 * j + jj) >= cc).astype(np.float32)
    return c


def _p1_inputs(inp, b, g):
    w_in = inp["w_in"][0]
    sl = slice(256 * g, 256 * (g + 1))
    cols = [w_in[:, 0 * 1024:1 * 1024][:, sl], w_in[:, 1 * 1024:2 * 1024][:, sl], w_in[:, 2 * 1024:3 * 1024][:, sl],
            w_in[:, 3 * 1024:4 * 1024][:, sl], w_in[:, 4 * 1024:5 * 1024][:, sl]]
    w1 = np.ascontiguousarray(np.concatenate(cols, axis=1))
    vec = np.zeros((128, 16), np.float32)
    for ct in range(2):
        ch = slice(256 * g + 128 * ct, 256 * g + 128 * (ct + 1))
        for k in range(4):
            vec[:, 8 * ct + k] = inp["conv_w"][0, k, ch]
        vec[:, 8 * ct + 4] = inp["conv_b"][0, ch]
        vec[:, 8 * ct + 5] = inp["b_rgate"][0, ch]
        vec[:, 8 * ct + 6] = inp["b_igate"][0, ch]
        vec[:, 8 * ct + 7] = inp["lru_lambda"][0, ch]
    return {
        "xT": np.ascontiguousarray(inp["x"][b].T),
        "w1": w1,
        "vec1": vec,
        "wr": np.ascontiguousarray(inp["w_rgate"][0, 4 * g:4 * g + 4]),
        "wi": np.ascontiguousarray(inp["w_igate"][0, 4 * g:4 * g + 4]),
        "gm": np.ascontiguousarray(inp["norm_mix_g"][0].reshape(8, 128).T),
        "cst": _consts(),
    }


def build_p1_program(dbg=False):
    nc = bass.Bass("TRN2", target_bir_lowering=False)
    dr = {}
    if dbg:
        dr["dbgF"] = nc.dram_tensor("dbgF", [128, Ctx.FSZ], F32, kind="ExternalOutput").ap()
        dr["dbgB"] = nc.dram_tensor("dbgB", [128, Ctx.BSZ], BF16, kind="ExternalOutput").ap()
        dr["dbgR"] = nc.dram_tensor("dbgR", [128, Ctx.RSZ], BF16, kind="ExternalOutput").ap()
    for nm, shp in (("xT", [D, S]), ("w1", [D, 1280]), ("vec1", [128, 16]), ("wr", [4, 64, 64]), ("wi", [4, 64, 64]),
                    ("gm", [128, 8]), ("cst", [128, 2432])):
        dr[nm] = nc.dram_tensor(nm, shp, F32, kind="ExternalInput").ap()
    dr["yT"] = nc.dram_tensor("yT", [512, S], BF16, kind="ExternalOutput").ap()
    C = Ctx(nc)
    build_phase1(nc, C, dr)
    global LAST_LAY
    LAST_LAY = list(C.lay)
    C.P.emit()
    C.es.close()
    return nc


def run_phase1(inp):
    nc = build_p1_program()
    in_maps = [_p1_inputs(inp, c // 4, c % 4) for c in range(NCORE)]
    res = run_bass_kernel_spmd(nc, in_maps, core_ids=list(range(NCORE)))
    return [np.asarray(r["yT"]) for r in res.results]


P2_IN = (("xT2", [D, 2048], F32), ("yX", [4, 512, 2048], BF16), ("pT", [256, 2048], F32), ("wg", [D, 2048], F32),
         ("wbl", [D, D], F32), ("wba", [D, D], F32), ("wo", [D, D], F32), ("wpg", [D, D], F32), ("wup", [D, 4096], F32),
         ("wdn", [4096, D], F32), ("wple", [256, D], F32), ("g2", [128, 32], F32))


def _p2_inputs(inp, b, g):
    tok = slice(2048 * g, 2048 * (g + 1))
    g2 = np.concatenate([inp[k].reshape(-1).reshape(8, 128).T for k in ("norm_mix_g", "norm_mlp_g", "norm_ple_g", "norm_final_g")], axis=1)
    return {
        "xT2": np.ascontiguousarray(inp["x"][b, tok].T),
        "pT": np.ascontiguousarray(inp["p"][0, b, tok].T),
        "wg": np.ascontiguousarray(inp["w_in"][0][:, 5120:7168]),
        "wbl": inp["w_br_lru"][0], "wba": inp["w_br_att"][0], "wo": inp["w_out"][0], "wpg": inp["w_ple_gate"][0],
        "wup": inp["w_mlp_up"][0], "wdn": inp["w_mlp_down"][0], "wple": inp["w_ple"][0],
        "g2": np.ascontiguousarray(g2),
    }


def build_p2_program():
    nc = bass.Bass("TRN2", target_bir_lowering=False)
    dr = {}
    for nm, shp, dt in P2_IN:
        dr[nm] = nc.dram_tensor(nm, shp, dt, kind="ExternalInput").ap()
    dr["oT"] = nc.dram_tensor("oT", [D, 2048], F32, kind="ExternalOutput").ap()
    C = Ctx(nc)
    C.mark()
    build_phase2(nc, C, dr)
    C.P.emit()
    C.es.close()
    return nc


def run_phase2(inp, ys):
    nc = build_p2_program()
    in_maps = []
    for c in range(NCORE):
        b, g = c // 4, c % 4
        m = _p2_inputs(inp, b, g)
        m["yX"] = np.ascontiguousarray(np.stack([ys[4 * b + r][:, 2048 * g:2048 * (g + 1)] for r in range(4)]))
        in_maps.append(m)
    res = run_bass_kernel_spmd(nc, in_maps, core_ids=list(range(NCORE)))
    return [np.asarray(r["oT"]) for r in res.results]


def build_fused_program():
    nc = bass.Bass("TRN2", target_bir_lowering=False)
    dr = {}
    for nm, shp in (("xT", [D, S]), ("w1", [D, 1280]), ("vec1", [128, 16]), ("wr", [4, 64, 64]), ("wi", [4, 64, 64]),
                    ("gm", [128, 8]), ("cst", [128, 2432])):
        dr[nm] = nc.dram_tensor(nm, shp, F32, kind="ExternalInput").ap()
    for nm, shp, dt in P2_IN:
        if nm != "yX":
            dr[nm] = nc.dram_tensor(nm, shp, dt, kind="ExternalInput").ap()
    dr["oT"] = nc.dram_tensor("oT", [D, 2048], F32, kind="ExternalOutput").ap()
    yin = [nc.dram_tensor("yin%d" % j, [512, 1024], BF16) for j in range(8)]
    yall = nc.dram_tensor("yall", [8 * 2048, 1024], BF16)
    dr["yin"] = [t.ap() for t in yin]
    dr["yin_tok"] = [[] for j in range(8)]
    dr["yall"] = yall.ap()
    dr["yall_r"] = [Res("yall%d" % j) for j in range(8)]
    dr["wsc_r"] = {}
    for nm, K_, N_ in WSPEC:
        dr[nm + "_b"] = nc.dram_tensor(nm + "_b", [K_, N_], BF16).ap()
        dr["wsc_r"][nm] = Res("wsc_" + nm)
    C = Ctx(nc)

    def issue_cc(j):
        C.P.op("pool", lambda e: e.collective_compute("AllGather", ALU.bypass, replica_groups=[[0, 1, 2, 3], [4, 5, 6, 7]],
                                                       ins=[yin[j].ap().opt()], outs=[yall.ap()[2048 * j:2048 * (j + 1), :].opt()]),
               extra=dr["yin_tok"][j], writes=[dr["yall_r"][j]], dma="ccag%d" % j, cc=True)
    dr["issue_cc"] = issue_cc
    build_phase1(nc, C, dr)
    build_phase2(nc, C, dr)
    C.P.emit()
    C.es.close()
    return nc


def kernel(**inp):
    inp = {k: np.asarray(v) for k, v in inp.items()}
    nc = build_fused_program()
    in_maps = []
    for c in range(NCORE):
        b, g = c // 4, c % 4
        m = _p1_inputs(inp, b, g)
        m.update(_p2_inputs(inp, b, g))
        in_maps.append(m)
    res = run_bass_kernel_spmd(nc, in_maps, core_ids=list(range(NCORE)))
    out = np.empty((2, S, D), np.float32)
    for c in range(NCORE):
        b, g = c // 4, c % 4
        out[b, 2048 * g:2048 * (g + 1), :] = np.asarray(res.results[c]["oT"]).T
    return out
```
